# Optimizing a Trainium2 kernel written in Bass

```python
import jax, jax.numpy as jnp
from jax import lax
import numpy as np

D_MODEL = 1024
BATCH = 2
SEQ = 8192
DEPTH = 2
DEC_BATCH = 32
DEC_SEQ = 2048
PAST_LEN = 128

GRID_W = 64
N_MEM = 256
D_MIX = 2 * D_MODEL
GROUP_W = D_MIX // 4
HEAD_DIM = 64
N_HEADS = GROUP_W // HEAD_DIM
N_KV_HEADS = 2
Q_PER_KV = N_HEADS // N_KV_HEADS
ROPE_PAIRS = HEAD_DIM // 4
ROPE_THETA = 10000.0
Q_BLOCK = 128
N_FOURIER_GROUPS = 4
FOURIER_W = GROUP_W // N_FOURIER_GROUPS
N_SGU_HEADS = 4
SGU_W = GROUP_W // N_SGU_HEADS
CHUNK = 128
N_MEM_HEADS = 4
MEM_HEAD_DIM = GROUP_W // N_MEM_HEADS
EPS = 1e-6
SPLIT_SIZES = (GROUP_W, N_KV_HEADS * HEAD_DIM, N_KV_HEADS * HEAD_DIM, GROUP_W,
               GROUP_W, GROUP_W, GROUP_W, GROUP_W, GROUP_W, GROUP_W, GROUP_W)
IN_W = sum(SPLIT_SIZES)
SPLIT_POINTS = tuple(int(v) for v in np.cumsum(SPLIT_SIZES)[:-1])

kernel_name = "hybrid_parallel_group_encoder"


def rms_norm(x, g):
    xf = x.astype(jnp.float32)
    y = xf * lax.rsqrt(jnp.mean(xf * xf, axis=-1, keepdims=True) + EPS)
    return (y * g.astype(jnp.float32)).astype(x.dtype)


def axial_rope_tables(S):
    rows = S // GRID_W
    row = jnp.broadcast_to(jnp.arange(rows, dtype=jnp.float32)[:, None], (rows, GRID_W)).reshape(S)
    col = jnp.broadcast_to(jnp.arange(GRID_W, dtype=jnp.float32)[None, :], (rows, GRID_W)).reshape(S)
    inv = ROPE_THETA ** (-jnp.arange(ROPE_PAIRS, dtype=jnp.float32) / ROPE_PAIRS)
    ang = jnp.stack([row[:, None] * inv, col[:, None] * inv], axis=1)
    return jnp.cos(ang), jnp.sin(ang)


def apply_axial_rope(x, cos, sin):
    B, S, H, _ = x.shape
    xr = x.astype(jnp.float32).reshape(B, S, H, 2, 2, ROPE_PAIRS)
    x1, x2 = xr[..., 0, :], xr[..., 1, :]
    c = cos[None, :, None]
    s = sin[None, :, None]
    out = jnp.stack([x1 * c - x2 * s, x2 * c + x1 * s], axis=-2)
    return out.reshape(B, S, H, HEAD_DIM).astype(x.dtype)


def self_attention(q, k, v):
    B, S = q.shape[0], q.shape[1]
    nblk = S // Q_BLOCK
    scale = HEAD_DIM ** -0.5
    qb = q.reshape(B, nblk, Q_BLOCK, N_KV_HEADS, Q_PER_KV, HEAD_DIM).transpose(1, 0, 2, 3, 4, 5)

    def one_block(qblk):
        s = jnp.einsum('bqkgd,bskd->bkgqs', qblk, k, preferred_element_type=jnp.float32) * scale
        p = jax.nn.softmax(s, axis=-1)
        return jnp.einsum('bkgqs,bskd->bqkgd', p.astype(v.dtype), v)

    o = lax.map(one_block, qb)
    return o.transpose(1, 0, 2, 3, 4, 5).reshape(B, S, N_HEADS * HEAD_DIM)


def fourier_mix(a, w_f):
    B, S, _ = a.shape
    ag = a.astype(jnp.float32).reshape(B, S, N_FOURIER_GROUPS, FOURIER_W)
    f = jnp.fft.fft2(ag, axes=(1, 3), norm='ortho').real
    y = jnp.einsum('bsgc,gcd->bsgd', f, w_f.astype(jnp.float32))
    return y.reshape(B, S, GROUP_W).astype(a.dtype)


def spatial_gating(u, vv, v_g, w_s, b_s):
    B, S, _ = u.shape
    vh = rms_norm(vv.reshape(B, S, N_SGU_HEADS, SGU_W), v_g)
    vc = vh.reshape(B, S // CHUNK, CHUNK, N_SGU_HEADS, SGU_W)
    sp = jnp.einsum('hpq,bnqhc->bnphc', w_s, vc) + b_s.T[None, None, :, :, None]
    return u * sp.reshape(B, S, GROUP_W)


def memory_attention(cq, mem, mem_g, w_mem_kv):
    B, S, _ = cq.shape
    kv = rms_norm(mem, mem_g) @ w_mem_kv
    mk, mv = jnp.split(kv, 2, axis=-1)
    M = mem.shape[1]
    mk = mk.reshape(B, M, N_MEM_HEADS, MEM_HEAD_DIM)
    mv = mv.reshape(B, M, N_MEM_HEADS, MEM_HEAD_DIM)
    qh = cq.reshape(B, S, N_MEM_HEADS, MEM_HEAD_DIM)
    s = jnp.einsum('bshd,bmhd->bhsm', qh, mk, preferred_element_type=jnp.float32) * (MEM_HEAD_DIM ** -0.5)
    p = jax.nn.softmax(s, axis=-1)
    o = jnp.einsum('bhsm,bmhd->bshd', p.astype(mv.dtype), mv)
    return o.reshape(B, S, GROUP_W)


def hybrid_layer(x, mem, cos, sin, pre_g, w_in, q_g, k_g, w_f, v_g, w_s, b_s, mem_g, w_mem_kv, w_out, post_g):
    B, S, _ = x.shape
    h = rms_norm(x, pre_g)
    z = h @ w_in
    (aq, ak, av, ag, fa, fg, su, sv, sg, mq, mg) = jnp.split(z, SPLIT_POINTS, axis=-1)
    q = rms_norm(aq.reshape(B, S, N_HEADS, HEAD_DIM), q_g)
    k = rms_norm(ak.reshape(B, S, N_KV_HEADS, HEAD_DIM), k_g)
    v = av.reshape(B, S, N_KV_HEADS, HEAD_DIM)
    q = apply_axial_rope(q, cos, sin)
    k = apply_axial_rope(k, cos, sin)
    o_att = self_attention(q, k, v) * jax.nn.silu(ag)
    o_four = fourier_mix(fa, w_f) * jax.nn.silu(fg)
    o_sgu = spatial_gating(su, sv, v_g, w_s, b_s) * jax.nn.silu(sg)
    o_mem = memory_attention(mq, mem, mem_g, w_mem_kv) * jax.nn.silu(mg)
    o = jnp.concatenate([o_att, o_four, o_sgu, o_mem], axis=-1) @ w_out
    return x + rms_norm(o, post_g)


def trunk(x, mem, pre_norm_g, w_in, q_norm_g, k_norm_g, w_fourier, sgu_norm_g, w_spatial,
          b_spatial, mem_norm_g, w_mem_kv, w_out, post_norm_g):
    cos, sin = axial_rope_tables(x.shape[1])
    for l in range(DEPTH):
        x = hybrid_layer(x, mem, cos, sin, pre_norm_g[l], w_in[l], q_norm_g[l], k_norm_g[l],
                         w_fourier[l], sgu_norm_g[l], w_spatial[l], b_spatial[l],
                         mem_norm_g[l], w_mem_kv[l], w_out[l], post_norm_g[l])
    return x


def setup_inputs(seed: int = 0) -> dict:
    key = jax.random.key(seed)
    ks = jax.random.split(key, 17)
    f32 = jnp.float32
    nrm = lambda k, shp, s: jax.random.normal(k, shp, f32) * s
    gain = lambda k, shp: 1.0 + 0.02 * jax.random.normal(k, shp, f32)
    return {
        "x_prompt": nrm(ks[0], (BATCH, SEQ, D_MODEL), 1.0),
        "x_sample": nrm(ks[1], (DEC_BATCH, DEC_SEQ, D_MODEL), 1.0),
        "mem_prompt": nrm(ks[2], (BATCH, N_MEM, D_MODEL), 1.0),
        "mem_sample": nrm(ks[3], (DEC_BATCH, N_MEM, D_MODEL), 1.0),
        "pre_norm_g": gain(ks[4], (DEPTH, D_MODEL)),
        "w_in": nrm(ks[5], (DEPTH, D_MODEL, IN_W), D_MODEL ** -0.5),
        "q_norm_g": gain(ks[6], (DEPTH, HEAD_DIM)),
        "k_norm_g": gain(ks[7], (DEPTH, HEAD_DIM)),
        "w_fourier": nrm(ks[8], (DEPTH, N_FOURIER_GROUPS, FOURIER_W, FOURIER_W), FOURIER_W ** -0.5),
        "sgu_norm_g": gain(ks[9], (DEPTH, N_SGU_HEADS, SGU_W)),
        "w_spatial": nrm(ks[10], (DEPTH, N_SGU_HEADS, CHUNK, CHUNK), CHUNK ** -0.5),
        "b_spatial": nrm(ks[11], (DEPTH, N_SGU_HEADS, CHUNK), 0.02),
        "mem_norm_g": gain(ks[12], (DEPTH, D_MODEL)),
        "w_mem_kv": nrm(ks[13], (DEPTH, D_MODEL, 2 * GROUP_W), D_MODEL ** -0.5),
        "w_out": nrm(ks[14], (DEPTH, D_MIX, D_MODEL), D_MIX ** -0.5),
        "post_norm_g": gain(ks[15], (DEPTH, D_MODEL)),
    }


def reference(x_prompt, x_sample, mem_prompt, mem_sample, pre_norm_g, w_in, q_norm_g, k_norm_g,
              w_fourier, sgu_norm_g, w_spatial, b_spatial, mem_norm_g, w_mem_kv, w_out, post_norm_g):
    y_prompt = trunk(x_prompt, mem_prompt, pre_norm_g, w_in, q_norm_g, k_norm_g, w_fourier,
                     sgu_norm_g, w_spatial, b_spatial, mem_norm_g, w_mem_kv, w_out, post_norm_g)
    y_sample = trunk(x_sample, mem_sample, pre_norm_g, w_in, q_norm_g, k_norm_g, w_fourier,
                     sgu_norm_g, w_spatial, b_spatial, mem_norm_g, w_mem_kv, w_out, post_norm_g)
    return (y_prompt, y_sample)
```

```python
import numpy as np
import ml_dtypes
import concourse.bass as bass
import concourse.mybir as mybir
from concourse.bass_utils import run_bass_kernel_spmd

F32 = mybir.dt.float32
BF16 = mybir.dt.bfloat16
ALU = mybir.AluOpType
AF = mybir.ActivationFunctionType

D = 1024
DC = 8
INW = 4864
NB = 512
EPS = 1e-6
NMEM = 256
SAME_ENG_SYNC = True

O_AQ, O_AK, O_AV, O_AG, O_FA, O_FG, O_SU, O_SV, O_SG, O_MQ, O_MG = (
    0, 512, 640, 768, 1280, 1792, 2304, 2816, 3328, 3840, 4352)
TQ, TK, TAG, TFA, TFG, TSU, TSG, TMQ, TMG, TV = 0, 4, 5, 9, 13, 17, 21, 25, 29, 33
NTA = 34


def _tile_cols(i):
    if i < 4:
        return [(O_AQ + 64 * i, 64), (O_AQ + 256 + 64 * i, 64)]
    if i == TK:
        return [(O_AK, 128)]
    if i == TV:
        return [(O_AV, 128)]
    for base, off in ((TAG, O_AG), (TFA, O_FA), (TFG, O_FG), (TSU, O_SU), (TSG, O_SG), (TMQ, O_MQ), (TMG, O_MG)):
        if base <= i < base + 4:
            return [(off + 128 * (i - base), 128)]
    raise ValueError(i)


class Res:
    __slots__ = ("name", "ws", "rs", "prev", "multi")

    def __init__(self, name, multi=False):
        self.name = name
        self.ws = {}
        self.rs = {}
        self.prev = {}
        self.multi = multi


def _merge(d, toks):
    for s, v in toks.items():
        if d.get(s, 0) < v:
            d[s] = v


class Trk:
    def __init__(self, nc):
        self.nc = nc
        self.eng = {"pe": nc.tensor, "act": nc.scalar, "dve": nc.vector, "pool": nc.gpsimd, "sp": nc.sync}
        self.sem = {k: nc.alloc_semaphore("es_" + k) for k in self.eng}
        self.cnt = {k: 0 for k in self.eng}
        self.waited = {k: {} for k in self.eng}
        self.ring = {}
        self.ring_i = {}
        for q in ("sp", "pool", "act"):
            self.ring[q] = [[nc.alloc_semaphore("dq_%s_%d" % (q, i)), 0] for i in range(12)]
            self.ring_i[q] = 0
        self.n_inst = 0

    def _deps(self, reads, writes):
        toks = {}
        for r in reads:
            _merge(toks, r.ws)
        for w in writes:
            if w.rs:
                _merge(toks, w.rs)
                _merge(toks, w.ws)
            else:
                _merge(toks, w.prev)
                if not w.multi:
                    _merge(toks, w.ws)
        return toks

    def _commit(self, reads, writes, tok):
        for w in writes:
            if w.rs:
                w.prev = {}
                _merge(w.prev, w.rs)
                _merge(w.prev, w.ws)
                w.ws = {}
                w.rs = {}
            if not w.multi:
                w.ws = {}
            _merge(w.ws, tok)
        for r in reads:
            if r not in writes:
                _merge(r.rs, tok)

    def _wait(self, en, toks):
        e = self.eng[en]
        wd = self.waited[en]
        for s, v in toks.items():
            if s is self.sem[en] and (en == "pe" or not SAME_ENG_SYNC):
                continue
            if wd.get(s, 0) >= v:
                continue
            e.wait_ge(s, v)
            wd[s] = v
            self.n_inst += 1

    def op(self, en, fn, reads=(), writes=(), sig=True):
        toks = self._deps(reads, writes)
        self._wait(en, toks)
        inst = fn(self.eng[en])
        self.n_inst += 1
        if sig:
            self.cnt[en] += 1
            inst.then_inc(self.sem[en], 1)
            tok = {self.sem[en]: self.cnt[en]}
        else:
            tok = {self.sem[en]: self.cnt[en] + 1}
        self._commit(reads, writes, tok)
        return inst

    def dma(self, q, out, in_, reads=(), writes=()):
        toks = self._deps(reads, writes)
        slot = self.ring[q][self.ring_i[q] % len(self.ring[q])]
        self.ring_i[q] += 1
        if slot[1]:
            toks = dict(toks)
            _merge(toks, {slot[0]: slot[1]})
        self._wait(q, toks)
        inst = self.eng[q].dma_start(out=out, in_=in_)
        self.n_inst += 1
        slot[1] += 16
        inst.then_inc(slot[0], 16)
        tok = {slot[0]: slot[1]}
        self._commit(reads, writes, tok)
        return tok

    def wait_all(self, en, ress):
        toks = {}
        for r in ress:
            _merge(toks, r.ws)
            _merge(toks, r.rs)
        self._wait(en, toks)


class Stream:
    def __init__(self, slots, loaders, depth=None):
        self.slots = slots
        self.loaders = loaders
        self.issued = 0
        self.depth = depth or len(slots)

    def get(self, i):
        while self.issued < min(len(self.loaders), i + self.depth):
            j = self.issued
            t, r = self.slots[j % len(self.slots)]
            self.loaders[j](t, r)
            self.issued += 1
        return self.slots[i % len(self.slots)]


def build(cfg):
    L = cfg["depth"]
    T = cfg["T"]
    NBLK = T // NB
    NTT = T // 128
    has_p = cfg["prompt"]
    NS = cfg["n_samp"]
    NCH = NS + (1 if has_p else 0)
    SP = 4 * T
    nc = bass.Bass("TRN2", target_bir_lowering=False)
    tk = Trk(nc)

    def din(name, shape, dt=F32):
        return nc.dram_tensor(name, list(shape), dt, kind="ExternalInput").ap()

    x_d = din("x", [NCH, T, D])
    mem_d = din("mem", [NCH, NMEM, D])
    w_in_d = din("w_in", [L, D, INW])
    w_out_d = din("w_out", [L, 2048, D])
    w_mkv_d = din("w_mem_kv", [L, D, 1024])
    w_four_d = din("w_fourier", [L, 4, 128, 128])
    w_spat_d = din("w_spatial", [L, 4, 128, 128])
    gcols_d = din("gcols", [L, 128, 32])
    vgb_d = din("vgb", [L, 128, 512])
    bsb_d = din("bsb", [L, 128, 4, NB])
    cst_d = din("cst", [5, 128, 128])
    rope_d = din("rope", [2, 2, 128, T])
    dfts_d = din("dft_s", [2, T, T], BF16)
    if has_p:
        dftp_d = din("dft_p", [2, SP, T], BF16)
    y_d = nc.dram_tensor("y", [NCH, T, D], F32, kind="ExternalOutput").ap()

    def dscr(name, shape, dt=BF16):
        return nc.dram_tensor(name, list(shape), dt).ap()

    wA_d = dscr("wA", [L, NTA, 128, DC, 128])
    wSV_d = dscr("wSV", [L, 128, DC, 512])
    wO_d = dscr("wO", [L, 8, 128, 16, 128])
    wMK_d = dscr("wMK", [L, 4, 128, DC, 128])
    wMV_d = dscr("wMV", [L, 128, DC, 512])
    wcws_d = dscr("wcws", [L, 128, 4, 256])
    wsT_d = dscr("wsT", [L, 128, 4, 128])
    kbuf_d = dscr("kbuf", [128, T])
    vbuf_d = [dscr("vbuf%d" % i, [NB, 320]) for i in range(NBLK)]
    uvbuf_d = [dscr("uvbuf%d" % i, [NB, 1024]) for i in range(NBLK)]
    cf_d = dscr("cfbuf", [128, 4, T])
    if has_p:
        kg_d = dscr("kg", [4 * 128, T])
        vg_d = [dscr("vg%d" % i, [4 * NB, 320]) for i in range(NBLK)]
        uvg_d = [dscr("uvg%d" % i, [4 * NB, 1024]) for i in range(NBLK)]
    r_wscr = Res("wscr", multi=True)
    r_kbuf, r_vbuf, r_uvbuf, r_cf = Res("kbuf", True), Res("vbuf", True), Res("uvbuf", True), Res("cf", True)
    r_kg, r_vg, r_uvg = Res("kg", True), Res("vg", True), Res("uvg", True)
    r_y = Res("y", True)

    SB_LO = 16512
    SB_HI = 229344
    st = {"p": SB_LO, "ov0": None, "ov": None, "ovmax": 0, "live": []}
    cnt = [0]

    def _bytes(shape, dt):
        n = 1
        for s in shape[1:]:
            n *= s
        return n * (2 if dt == BF16 else 4)

    def _alloc(name, shape, dt, off):
        cnt[0] += 1
        return nc.alloc_sbuf_tensor_at("%s_%d" % (name, cnt[0]), list(shape), dt, offset=off)

    def persist(name, shape, dt=F32, multi=False):
        assert st["ov0"] is None
        b = (_bytes(shape, dt) + 31) // 32 * 32
        t = _alloc(name, shape, dt, st["p"])
        st["p"] += b
        assert st["p"] <= SB_HI, "sbuf overflow persist %s" % name
        return t, Res(name, multi)

    def ov_begin():
        st["ov0"] = st["p"]
        st["ov"] = st["p"]

    def stage():
        inh = {}
        for r in st["live"]:
            _merge(inh, r.ws)
            _merge(inh, r.rs)
            _merge(inh, r.prev)
        _merge(inh, st.get("inh", {}))
        st["inh"] = inh
        st["live"] = []
        st["ov"] = st["ov0"]

    def ovl(name, shape, dt=F32, multi=False):
        b = (_bytes(shape, dt) + 31) // 32 * 32
        t = _alloc(name, shape, dt, st["ov"])
        st["ov"] += b
        st["ovmax"] = max(st["ovmax"], st["ov"])
        assert st["ov"] <= SB_HI, "sbuf overflow overlay %s (%d)" % (name, st["ov"])
        r = Res(name, multi)
        r.prev = dict(st.get("inh", {}))
        st["live"].append(r)
        return t, r

    xT, r_xT_all = persist("xT", [128, DC, T], F32, multi=True)
    r_xTb = [Res("xT_b%d" % i, multi=True) for i in range(NBLK)]
    xb, r_xb = persist("xb", [128, DC, NB], BF16, multi=True)
    rbc_all, r_rbc = persist("rbc_all", [128, NBLK, NB], F32, multi=True)
    rcol_all, r_rcol = persist("rcol_all", [128, NBLK * 4], F32, multi=True)
    cur = {}
    cosb, r_cos = persist("cosb", [128, NB])
    sinb, r_sin = persist("sinb", [128, NB])
    wcol = [persist("wcol%d" % i, [128, DC, 128], BF16) for i in range(4)]
    wbig = [persist("wbig%d" % i, [128, DC, 512], BF16) for i in range(1)]
    tmp = [persist("tmp%d" % i, [128, NB]) for i in range(6)]
    sqb = [persist("sqb%d" % i, [128, NB], BF16) for i in range(2)]
    concat, r_concat = persist("concat", [128, 16, NB], BF16, multi=True)
    KmT, r_KmT = persist("KmT", [128, 4, NMEM], BF16, multi=True)
    Vm, r_Vm = persist("Vm", [128, 2, 512], BF16, multi=True)
    rsm, r_rsm = persist("rsm", [128, 2])
    ident, r_ident = persist("ident", [128, 128])
    rotT, r_rotT = persist("rotT", [128, 128])
    bones, r_bones = persist("bones", [128, 128], BF16)
    onesb, r_onesb = persist("onesb", [128, 128], BF16)
    onesf, r_onesf = persist("onesf", [128, 128])
    c128, r_c128 = persist("c128", [128, 128])
    s128, r_s128 = persist("s128", [128, 128])
    epsc, r_epsc = persist("epsc", [128, 1])
    sel0, r_sel0 = persist("sel0", [128, 128])
    sel64, r_sel64 = persist("sel64", [128, 128])
    selb0, r_selb0 = persist("selb0", [128, 128], BF16)
    selb64, r_selb64 = persist("selb64", [128, 128], BF16)
    wcws, r_wcws = persist("wcws", [128, 4, 256], BF16)
    wsT, r_wsT = persist("wsT", [128, 4, 128], BF16)
    bsb, r_bsb = persist("bsb", [128, 4, NB])
    vgb, r_vgb = persist("vgb", [128, 512])
    gcol, r_gcol = persist("gcol", [128, 32])
    ginv, r_ginv = persist("ginv", [128, 2])
    tmp_i = [0]
    ov_begin()

    ps = nc.alloc_psum_tensor("ps", [128, 8, NB], F32)
    r_ps = [Res("ps%d" % i, multi=True) for i in range(8)]

    def nexttmp():
        tmp_i[0] += 1
        return tmp[tmp_i[0] % len(tmp)]

    sq_i = [0]

    def nextsq():
        sq_i[0] += 1
        return sqb[sq_i[0] % 2]

    pj_i = [0]

    def nextpj():
        pj_i[0] += 1
        return pj_i[0] % 2

    def mm(out, lhsT, rhs, start, stop, reads, w, sig=None):
        tk.op("pe", lambda e: e.matmul(out, lhsT=lhsT, rhs=rhs, start=start, stop=stop),
              reads=reads, writes=[w], sig=stop if sig is None else sig)

    def tr(out, in_, reads, w, sig=True):
        tk.op("pe", lambda e: e.transpose(out, in_, ident[:]), reads=list(reads) + [r_ident], writes=[w], sig=sig)

    stage()
    cst_f, r_cstf = ovl("cst_f", [128, 5, 128])
    tk.dma("sp", cst_f[:], cst_d.rearrange("k p n -> p k n"), writes=[r_cstf])
    tk.op("dve", lambda e: e.tensor_copy(out=ident[:], in_=cst_f[:, 0, :]), reads=[r_cstf], writes=[r_ident])
    tk.op("dve", lambda e: e.tensor_copy(out=rotT[:], in_=cst_f[:, 1, :]), reads=[r_cstf], writes=[r_rotT])
    tk.op("dve", lambda e: e.tensor_copy(out=bones[:], in_=cst_f[:, 2, :]), reads=[r_cstf], writes=[r_bones])
    tk.op("dve", lambda e: e.tensor_copy(out=c128[:], in_=cst_f[:, 3, :]), reads=[r_cstf], writes=[r_c128])
    tk.op("dve", lambda e: e.tensor_copy(out=s128[:], in_=cst_f[:, 4, :]), reads=[r_cstf], writes=[r_s128])
    tk.op("pool", lambda e: e.memset(onesb[:], 1.0), writes=[r_onesb])
    tk.op("pool", lambda e: e.memset(onesf[:], 1.0), writes=[r_onesf])
    tk.op("pool", lambda e: e.memset(epsc[:], EPS), writes=[r_epsc])
    tk.op("pool", lambda e: e.memset(sel0[:], 0.0), writes=[r_sel0])
    tk.op("pool", lambda e: e.memset(sel0[0:1, :], 1.0), writes=[r_sel0])
    tk.op("pool", lambda e: e.memset(sel64[:], 0.0), writes=[r_sel64])
    tk.op("pool", lambda e: e.memset(sel64[64:65, :], 1.0), writes=[r_sel64])
    tk.op("dve", lambda e: e.tensor_copy(out=selb0[:], in_=sel0[:]), reads=[r_sel0], writes=[r_selb0])
    tk.op("dve", lambda e: e.tensor_copy(out=selb64[:], in_=sel64[:]), reads=[r_sel64], writes=[r_selb64])

    def prep_layer(l):
        stage()
        gexp, r_gexp = ovl("gexp", [128, DC, 128])
        gmexp, r_gmexp = ovl("gmexp", [128, DC, 128])
        stg = [ovl("stg%d" % i, [128, DC, 128]) for i in range(2)]
        stb = [ovl("stb%d" % i, [128, DC, 128], BF16) for i in range(2)]
        sto = [ovl("sto%d" % i, [128, 16, 128]) for i in range(2)]
        stob = [ovl("stob%d" % i, [128, 16, 128], BF16) for i in range(2)]
        wf, r_wf = ovl("wf", [128, 4, 128])
        wsp, r_wsp = ovl("wsp", [128, 4, 128])
        wcws_s, r_wcws_s = ovl("wcws_s", [128, 4, 256], BF16, multi=True)
        wsT_s, r_wsT_s = ovl("wsT_s", [128, 4, 128], BF16, multi=True)
        tk.dma("sp", gcol[:], gcols_d[l], writes=[r_gcol])
        for c in range(DC):
            tk.op("dve", lambda e, c=c: e.tensor_scalar(out=gexp[:, c, :], in0=onesf[:], scalar1=gcol[:, c:c + 1], scalar2=None, op0=ALU.mult),
                  reads=[r_onesf, r_gcol], writes=[r_gexp] if c == 0 else [r_gexp])
            tk.op("dve", lambda e, c=c: e.tensor_scalar(out=gmexp[:, c, :], in0=onesf[:], scalar1=gcol[:, 16 + c:17 + c], scalar2=None, op0=ALU.mult),
                  reads=[r_onesf, r_gcol], writes=[r_gmexp])
        r_gexp.multi = True
        r_gmexp.multi = True
        n = [0]

        def conv(src_aps, dst_ap, gx, r_gx):
            i = n[0] % 2
            n[0] += 1
            (sg, r_sg), (sbb, r_sbb) = stg[i], stb[i]
            r_sg.multi = True
            off = 0
            for a, w_ in src_aps:
                tk.dma("sp", sg[:, :, off:off + w_], a, writes=[r_sg])
                off += w_
            if gx is None:
                tk.op("dve", lambda e: e.tensor_copy(out=sbb[:], in_=sg[:]), reads=[r_sg], writes=[r_sbb])
            else:
                tk.op("dve", lambda e: e.tensor_tensor(out=sbb[:], in0=sg[:], in1=gx[:], op=ALU.mult), reads=[r_sg, r_gx], writes=[r_sbb])
            tk.dma("pool", dst_ap, sbb[:], reads=[r_sbb], writes=[r_wscr])

        win_v = w_in_d[l].rearrange("(c p) n -> p c n", p=128)
        for i in range(NTA):
            conv([(win_v[:, :, a:a + w_], w_) for a, w_ in _tile_cols(i)], wA_d[l, i], gexp, r_gexp)
        for j in range(4):
            conv([(win_v[:, :, O_SV + 128 * j:O_SV + 128 * (j + 1)], 128)], wSV_d[l][:, :, 128 * j:128 * (j + 1)], gexp, r_gexp)
        wm_v = w_mkv_d[l].rearrange("(c p) n -> p c n", p=128)
        for h in range(4):
            conv([(wm_v[:, :, 128 * h:128 * (h + 1)], 128)], wMK_d[l, h], gmexp, r_gmexp)
        for j in range(4):
            conv([(wm_v[:, :, 512 + 128 * j:512 + 128 * (j + 1)], 128)], wMV_d[l][:, :, 128 * j:128 * (j + 1)], gmexp, r_gmexp)
        wo_v = w_out_d[l].rearrange("(c p) n -> p c n", p=128)
        for dt in range(8):
            (so, r_so), (sob, r_sob) = sto[dt % 2], stob[dt % 2]
            tk.dma("sp", so[:], wo_v[:, :, 128 * dt:128 * (dt + 1)], writes=[r_so])
            tk.op("dve", lambda e: e.tensor_copy(out=sob[:], in_=so[:]), reads=[r_so], writes=[r_sob])
            tk.dma("pool", wO_d[l, dt], sob[:], reads=[r_sob], writes=[r_wscr])
        tk.dma("sp", wf[:], w_four_d[l].rearrange("g j c -> j g c"), writes=[r_wf])
        tk.dma("sp", wsp[:], w_spat_d[l].rearrange("h p q -> p h q"), writes=[r_wsp])
        for g in range(4):
            mm(ps[:, 6, 0:128], c128[:], wf[:, g, :], True, True, [r_c128, r_wf], r_ps[6])
            mm(ps[:, 6, 128:256], s128[:], wf[:, g, :], True, True, [r_s128, r_wf], r_ps[6])
            tk.op("dve", lambda e, g=g: e.tensor_copy(out=wcws_s[:, g, :], in_=ps[:, 6, 0:256]), reads=[r_ps[6]], writes=[r_wcws_s])
            tr(ps[:, 7, 0:128], wsp[:, g, :], [r_wsp], r_ps[7])
            tk.op("dve", lambda e, g=g: e.tensor_copy(out=wsT_s[:, g, :], in_=ps[:, 7, 0:128]), reads=[r_ps[7]], writes=[r_wsT_s])
        tk.dma("pool", wcws_d[l], wcws_s[:], reads=[r_wcws_s], writes=[r_wscr])
        tk.dma("pool", wsT_d[l], wsT_s[:], reads=[r_wsT_s], writes=[r_wscr])

    for l in range(L):
        prep_layer(l)

    def load_layer_consts(l):
        tk.dma("sp", gcol[:], gcols_d[l], writes=[r_gcol])
        tk.dma("sp", wcws[:], wcws_d[l], reads=[r_wscr], writes=[r_wcws])
        tk.dma("sp", wsT[:], wsT_d[l], reads=[r_wscr], writes=[r_wsT])
        tk.dma("sp", bsb[:], bsb_d[l], writes=[r_bsb])
        tk.dma("sp", vgb[:], vgb_d[l], writes=[r_vgb])
        tk.op("dve", lambda e: e.reciprocal(out=ginv[:], in_=gcol[:, 24:26]), reads=[r_gcol], writes=[r_ginv])

    P3_IDS = ([TQ + j for j in range(4)] + [TAG + j for j in range(4)] + [TSU + j for j in range(4)] + [TSG + j for j in range(4)]
              + [TMQ + j for j in range(4)] + [TMG + j for j in range(4)])
    G_IDS = []
    for _ch in range(NCH):
        for _l in range(L):
            for _b in range(NBLK):
                G_IDS += [(_l, i) for i in [TK, TV] + [TFA + j for j in range(4)]]
            for _b in range(NBLK):
                G_IDS += [(_l, TFG + j) for j in range(4)]
            for _b in range(NBLK):
                G_IDS += [(_l, i) for i in P3_IDS]

    def _mk_w(li):
        def ld(t, r):
            tk.dma("sp", t[:], wA_d[li[0], li[1]], reads=[r_wscr], writes=[r])
        return ld
    g_ws = Stream(wcol, [_mk_w(li) for li in G_IDS])
    g_pos = [0]

    class _WS:
        def __init__(self, l, ids):
            self.base = g_pos[0]
            for k, i in enumerate(ids):
                assert G_IDS[self.base + k] == (l, i), (self.base, k, G_IDS[self.base + k], (l, i))
            g_pos[0] += len(ids)

        def get(self, j):
            return g_ws.get(self.base + j)

    def wcol_stream(l, ids):
        return _WS(l, ids)

    def make_xb(b, stats=False):
        cur["rbc"] = rbc_all[:, b, :]
        cur["rcol"] = rcol_all[:, 4 * b:4 * b + 4]
        for c in range(DC):
            src = xT[:, c, b * NB:(b + 1) * NB]
            if c % 2 == 0:
                tk.op("dve", lambda e, c=c, src=src: e.tensor_copy(out=xb[:, c, :], in_=src), reads=[r_xTb[b]], writes=[r_xb])
            else:
                tk.op("act", lambda e, c=c, src=src: e.copy(out=xb[:, c, :], in_=src), reads=[r_xTb[b]], writes=[r_xb])
        if not stats:
            return
        for c in range(DC):
            sq, r_sq = nextsq()
            src = xT[:, c, b * NB:(b + 1) * NB]
            tk.op("act", lambda e, sq=sq, src=src: e.activation(out=sq[:], in_=src, func=AF.Square), reads=[r_xTb[b]], writes=[r_sq])
            mm(ps[:, 6, :], onesb[:], sq[:], c == 0, c == DC - 1, [r_onesb, r_sq], r_ps[6], sig=False)
            for tt in range(4):
                mm(ps[:, 7, tt:tt + 1], sq[:, tt * 128:(tt + 1) * 128], onesb[:, 0:1], c == 0, c == DC - 1, [r_onesb, r_sq], r_ps[7],
                   sig=(tt == 3))
        t1, r_t1 = nexttmp()
        tk.op("act", lambda e: e.activation(out=t1[:], in_=ps[:, 6, :], func=AF.Ln, bias=epsc[:], scale=1.0 / D), reads=[r_ps[6], r_epsc], writes=[r_t1])
        tk.op("act", lambda e: e.activation(out=cur["rbc"], in_=t1[:], func=AF.Exp, scale=-0.5), reads=[r_t1], writes=[r_rbc])
        t2, r_t2 = nexttmp()
        tk.op("act", lambda e: e.activation(out=t2[:, 0:4], in_=ps[:, 7, 0:4], func=AF.Ln, bias=epsc[:], scale=1.0 / D), reads=[r_ps[7], r_epsc], writes=[r_t2])
        tk.op("act", lambda e: e.activation(out=cur["rcol"], in_=t2[:, 0:4], func=AF.Exp, scale=-0.5), reads=[r_t2], writes=[r_rcol])

    def proj(wt, r_wt, bank):
        for c in range(DC):
            mm(ps[:, bank, :], wt[:, c, :], xb[:, c, :], c == 0, c == DC - 1, [r_wt, r_xb], r_ps[bank])

    def load_rope(pt, b):
        tk.dma("sp", cosb[:], rope_d[pt, 0][:, b * NB:(b + 1) * NB], writes=[r_cos])
        tk.dma("sp", sinb[:], rope_d[pt, 1][:, b * NB:(b + 1) * NB], writes=[r_sin])

    def qk_part_a(bank, gi, zg, r_zg, sq, r_sq):
        tk.op("dve", lambda e: e.scalar_tensor_tensor(out=zg[:], in0=ps[:, bank, :], scalar=gcol[:, 24 + gi:25 + gi], in1=cur["rbc"], op0=ALU.mult, op1=ALU.mult),
              reads=[r_ps[bank], r_gcol, r_rbc], writes=[r_zg])
        tk.op("act", lambda e: e.activation(out=sq[:], in_=zg[:], func=AF.Square, scale=ginv[:, gi:gi + 1]), reads=[r_zg, r_ginv], writes=[r_sq])

    def qk_part_b(zg, r_zg, sq, r_sq, ba, bb):
        mm(ps[:, ba, :], bones[:], sq[:], True, True, [r_bones, r_sq], r_ps[ba])
        mm(ps[:, bb, :], rotT[:], zg[:], True, True, [r_rotT, r_zg], r_ps[bb])

    def qk_part_c(zg, r_zg, ba, bb, outs, r_out):
        t1, r_t1 = nexttmp()
        tk.op("act", lambda e: e.activation(out=t1[:], in_=ps[:, ba, :], func=AF.Ln, bias=epsc[:], scale=1.0 / 64), reads=[r_ps[ba], r_epsc], writes=[r_t1])
        tk.op("act", lambda e: e.activation(out=t1[:], in_=t1[:], func=AF.Exp, scale=-0.5), reads=[r_t1], writes=[r_t1])
        t2, r_t2 = nexttmp()
        tk.op("dve", lambda e: e.tensor_tensor(out=t2[:], in0=ps[:, bb, :], in1=sinb[:], op=ALU.mult), reads=[r_ps[bb], r_sin], writes=[r_t2])
        tk.op("dve", lambda e: e.tensor_tensor(out=zg[:], in0=zg[:], in1=cosb[:], op=ALU.mult), reads=[r_zg, r_cos], writes=[r_zg])
        tk.op("dve", lambda e: e.tensor_tensor(out=t2[:], in0=t2[:], in1=zg[:], op=ALU.add), reads=[r_t2, r_zg], writes=[r_t2])
        for o_ap, rs_ in outs:
            tk.op("dve", lambda e, o_ap=o_ap, rs_=rs_: e.tensor_tensor(out=o_ap, in0=t2[rs_, :], in1=t1[rs_, :], op=ALU.mult), reads=[r_t2, r_t1], writes=[r_out])

    def gate_from(bank, g_ap, r_g):
        tk.op("dve", lambda e: e.tensor_tensor(out=g_ap, in0=ps[:, bank, :], in1=cur["rbc"], op=ALU.mult), reads=[r_ps[bank], r_rbc], writes=[r_g])
        tk.op("act", lambda e: e.activation(out=g_ap, in_=g_ap, func=AF.Silu), reads=[r_g], writes=[r_g])

    def mem_prep(ch, l):
        stage()
        mt = [ovl("memtok%d" % i, [128, D]) for i in range(2)]
        memT, r_memT = ovl("memT", [128, DC, NMEM], BF16, multi=True)
        junk, r_junk = ovl("junk", [128, D])
        ssm, r_ssm = ovl("ssm", [128, 2])
        wk = [ovl("wmk%d" % i, [128, DC, 128], BF16) for i in range(2)]
        wv, r_wv = ovl("wmv", [128, DC, 512], BF16)
        for mtile in range(2):
            m_t, r_m = mt[mtile]
            tk.dma("pool", m_t[:], mem_d[ch, mtile * 128:(mtile + 1) * 128, :], writes=[r_m])
            tk.op("act", lambda e, m_t=m_t, mtile=mtile: e.activation(out=junk[:], in_=m_t[:], func=AF.Square, accum_out=ssm[:, mtile:mtile + 1]),
                  reads=[r_m], writes=[r_junk, r_ssm])
            for c in range(DC):
                bank = 6 + (c % 2)
                tr(ps[:, bank, 0:128], m_t[:, c * 128:(c + 1) * 128], [r_m], r_ps[bank])
                tk.op("dve", lambda e, c=c, bank=bank, mtile=mtile: e.tensor_copy(out=memT[:, c, mtile * 128:(mtile + 1) * 128], in_=ps[:, bank, 0:128]),
                      reads=[r_ps[bank]], writes=[r_memT])
        r_ssm.multi = True
        t1, r_t1 = nexttmp()
        tk.op("act", lambda e: e.activation(out=t1[:, 0:2], in_=ssm[:], func=AF.Ln, bias=epsc[:], scale=1.0 / D), reads=[r_ssm, r_epsc], writes=[r_t1])
        tk.op("act", lambda e: e.activation(out=rsm[:], in_=t1[:, 0:2], func=AF.Exp, scale=-0.5), reads=[r_t1], writes=[r_rsm])
        tk.dma("sp", wv[:], wMV_d[l], reads=[r_wscr], writes=[r_wv])
        for mtile in range(2):
            bank = nextpj()
            for c in range(DC):
                mm(ps[:, bank, :], memT[:, c, mtile * 128:(mtile + 1) * 128], wv[:, c, :], c == 0, c == DC - 1, [r_memT, r_wv], r_ps[bank])
            tk.op("dve", lambda e, bank=bank, mtile=mtile: e.tensor_scalar(out=Vm[:, mtile, :], in0=ps[:, bank, :], scalar1=rsm[:, mtile:mtile + 1], scalar2=None, op0=ALU.mult),
                  reads=[r_ps[bank], r_rsm], writes=[r_Vm])
        for h in range(4):
            wkt, r_wk = wk[h % 2]
            tk.dma("sp", wkt[:], wMK_d[l, h], reads=[r_wscr], writes=[r_wk])
            bank = nextpj()
            for c in range(DC):
                mm(ps[:, bank, 0:NMEM], wkt[:, c, :], memT[:, c, :], c == 0, c == DC - 1, [r_wk, r_memT], r_ps[bank])
            tk.op("act", lambda e, bank=bank, h=h: e.copy(out=KmT[:, h, :], in_=ps[:, bank, 0:NMEM]), reads=[r_ps[bank]], writes=[r_KmT])
        tk.op("dve", lambda e: e.tensor_scalar(out=rsm[:], in0=rsm[:], scalar1=float(128 ** -0.5), scalar2=None, op0=ALU.mult), reads=[r_rsm, r_Vm], writes=[r_rsm])

    def phase1(l, pt):
        stage()
        faT, r_faT = ovl("faT", [128, 4, NB], BF16, multi=True)
        uvo = [ovl("uvo%d" % i, [128, 1024], BF16, multi=True) for i in range(2)]
        vao = [ovl("vao%d" % i, [128, 4, 320], BF16, multi=True) for i in range(2)]
        kto = [ovl("kto%d" % i, [128, NB], BF16) for i in range(2)]
        for i in range(2):
            tk.op("pool", lambda e, i=i: e.memset(vao[i][0][:], 1.0), writes=[vao[i][1]])
        for b in range(NBLK):
            make_xb(b, stats=True)
            load_rope(pt, b)
            ws = wcol_stream(l, [TK, TV] + [TFA + j for j in range(4)])
            wt, r_wt = ws.get(0)
            bank = nextpj()
            proj(wt, r_wt, bank)
            kt, r_kt = kto[b % 2]
            zgk, r_zgk = nexttmp()
            sqk, r_sqk = nextsq()
            qk_part_a(bank, 1, zgk, r_zgk, sqk, r_sqk)
            wt, r_wt = ws.get(1)
            va, r_va = vao[b % 2]
            for tt in range(4):
                bank = nextpj()
                for c in range(DC):
                    mm(ps[:, bank, 0:128], xb[:, c, tt * 128:(tt + 1) * 128], wt[:, c, :], c == 0, c == DC - 1, [r_xb, r_wt], r_ps[bank])
                for kv in range(2):
                    tk.op("dve", lambda e, bank=bank, tt=tt, kv=kv: e.tensor_scalar(out=va[:, tt, 64 + 128 * kv:128 + 128 * kv], in0=ps[:, bank, 64 * kv:64 * kv + 64],
                                                                                 scalar1=cur["rcol"][:, tt:tt + 1], scalar2=None, op0=ALU.mult),
                          reads=[r_ps[bank], r_rcol], writes=[r_va])
            tk.dma("pool", vbuf_d[b].rearrange("(t p) n -> p t n", p=128), va[:], reads=[r_va], writes=[r_vbuf])
            for j in range(4):
                wt, r_wt = ws.get(2 + j)
                bank = nextpj()
                proj(wt, r_wt, bank)
                tk.op("dve", lambda e, bank=bank, j=j: e.tensor_tensor(out=faT[:, j, :], in0=ps[:, bank, :], in1=cur["rbc"], op=ALU.mult),
                      reads=[r_ps[bank], r_rbc], writes=[r_faT])
            qk_part_b(zgk, r_zgk, sqk, r_sqk, 6, 7)
            qk_part_c(zgk, r_zgk, 6, 7, [(kt[:], slice(0, 128))], r_kt)
            tk.dma("pool", kbuf_d[:, b * NB:(b + 1) * NB], kt[:], reads=[r_kt], writes=[r_kbuf])
            for tt in range(4):
                uo, r_uo = uvo[tt % 2]
                for half in range(2):
                    bank = 2 + half
                    for g2 in range(2):
                        g = half * 2 + g2
                        mm(ps[:, bank, g2 * 256:(g2 + 1) * 256], faT[:, g, tt * 128:(tt + 1) * 128], wcws[:, g, :], True, True, [r_faT, r_wcws], r_ps[bank], sig=(g2 == 1))
                    if half == 0:
                        tk.op("act", lambda e, bank=bank, uo=uo: e.copy(out=uo[:, 0:512], in_=ps[:, bank, :]), reads=[r_ps[bank]], writes=[r_uo])
                    else:
                        tk.op("dve", lambda e, bank=bank, uo=uo: e.tensor_copy(out=uo[:, 512:1024], in_=ps[:, bank, :]), reads=[r_ps[bank]], writes=[r_uo])
                tk.dma("pool", uvbuf_d[b][tt * 128:(tt + 1) * 128, :], uo[:], reads=[r_uo], writes=[r_uvbuf])

    cc_sems = [[nc.alloc_semaphore("cc%d" % i), 0] for i in range(1 + 2 * NBLK)] if has_p else []

    def gather_prompt():
        def cc(k, i, o, r_i, r_o):
            toks = tk._deps([r_i], [r_o])
            tk._wait("pool", toks)
            inst = nc.gpsimd.collective_compute("AllGather", ALU.bypass, replica_groups=[[0, 1, 2, 3], [4, 5, 6, 7]], ins=[i.opt()], outs=[o.opt()])
            cc_sems[k][1] += 1
            inst.then_inc(cc_sems[k][0], 1)
            tk._commit([r_i], [r_o], {cc_sems[k][0]: cc_sems[k][1]})
        cc(0, kbuf_d, kg_d, r_kbuf, r_kg)
        for j in range(NBLK):
            cc(1 + j, vbuf_d[j], vg_d[j], r_vbuf, r_vg)
        for j in range(NBLK):
            cc(1 + NBLK + j, uvbuf_d[j], uvg_d[j], r_uvbuf, r_uvg)

    def phase2(l, is_p):
        stage()
        NT = (SP if is_p else T) // 128
        uvi = [ovl("uvi%d" % i, [128, 1024], BF16) for i in range(4)]
        csi = [ovl("csi%d" % i, [128, 2, 2 * NB], BF16) for i in range(4)]
        gf, r_gf = ovl("gf", [128, 2, 4, NB], F32, multi=True)
        cfos = [ovl("cfo%d" % i, [128, 4, NB], BF16, multi=True) for i in range(2)]
        r_uvsrc = r_uvg if is_p else r_uvbuf

        def uv_tile(tt):
            if is_p:
                r, lt = tt // NTT, tt % NTT
                j, i0 = lt // 4, (lt % 4) * 128
                return uvg_d[j][r * NB + i0:r * NB + i0 + 128, :]
            return uvbuf_d[tt // 4][(tt % 4) * 128:(tt % 4 + 1) * 128, :]
        dft_src = dftp_d if is_p else dfts_d
        for kbp in range(NBLK // 2):
            for kb2 in range(2):
                kb = 2 * kbp + kb2
                make_xb(kb)
                ws = wcol_stream(l, [TFG + j for j in range(4)])
                for j in range(4):
                    wt, r_wt = ws.get(j)
                    bank = nextpj()
                    proj(wt, r_wt, bank)
                    gate_from(bank, gf[:, kb2, j, :], r_gf)

            def mk_uv(tt):
                def ld(t, r):
                    tk.dma("sp", t[:], uv_tile(tt), reads=[r_uvsrc], writes=[r])
                return ld

            def mk_cs(tt, kbp=kbp):
                def ld(t, r):
                    tk.dma("sp", t[:], dft_src[:, tt * 128:(tt + 1) * 128, kbp * 2 * NB:(kbp + 1) * 2 * NB].rearrange("k p n -> p k n"), writes=[r])
                return ld
            s_uv = Stream(uvi, [mk_uv(tt) for tt in range(NT)])
            s_cs = Stream(csi, [mk_cs(tt) for tt in range(NT)])
            for tt in range(NT):
                u, r_u = s_uv.get(tt)
                cs, r_c = s_cs.get(tt)
                for kb2 in range(2):
                    for g in range(4):
                        bk = kb2 * 4 + g
                        mm(ps[:, bk, :], u[:, g * 256:g * 256 + 128], cs[:, 0, kb2 * NB:(kb2 + 1) * NB], tt == 0, False, [r_u, r_c], r_ps[bk], sig=False)
                        mm(ps[:, bk, :], u[:, g * 256 + 128:g * 256 + 256], cs[:, 1, kb2 * NB:(kb2 + 1) * NB], False, tt == NT - 1, [r_u, r_c], r_ps[bk],
                           sig=(tt == NT - 1 or (kb2 == 1 and g == 3)))
            for kb2 in range(2):
                kb = 2 * kbp + kb2
                cfo, r_cfo = cfos[kb2]
                for g in range(4):
                    bk = kb2 * 4 + g
                    tk.op("dve", lambda e, g=g, bk=bk, cfo=cfo, kb2=kb2: e.tensor_tensor(out=cfo[:, g, :], in0=ps[:, bk, :], in1=gf[:, kb2, g, :], op=ALU.mult),
                          reads=[r_ps[bk], r_gf], writes=[r_cfo])
                tk.dma("pool", cf_d[:, :, kb * NB:(kb + 1) * NB], cfo[:], reads=[r_cfo], writes=[r_cf])

    def phase3_block(l, b, pt, is_p):
        NKC = (SP if is_p else T) // 128
        if b == 0:
            make_xb(b)
        else:
            cur["rbc"] = rbc_all[:, b, :]
            cur["rcol"] = rcol_all[:, 4 * b:4 * b + 4]
        load_rope(pt, b)
        tk.dma("sp", concat[:, 4:8, :], cf_d[:, :, b * NB:(b + 1) * NB], reads=[r_cf], writes=[r_concat])
        stage()
        qrot, r_qrot0 = ovl("qrot", [128, 8, NB], BF16, multi=True)
        r_qt = [r_qrot0] + [Res("qrot_t%d" % i, multi=True) for i in range(1, 4)]
        for i in range(1, 4):
            r_qt[i].prev = dict(r_qrot0.prev)
            st["live"].append(r_qt[i])
        tk.op("pool", lambda e: e.memset(qrot[:], 0.0), writes=r_qt)
        Ga, r_Ga = ovl("Ga", [128, 4, NB], F32, multi=True)
        NKG_ = ((SP if is_p else T) // 128) // 4
        if is_p:
            kst = [ovl("kst%d" % i, [128, NB], BF16) for i in range(3)]
            vst = [ovl("vst%d" % i, [128, 4, 320], BF16) for i in range(3)]
        else:
            kst = [ovl("kres%d" % i, [128, NB], BF16) for i in range(NKG_)]
            vst = [ovl("vres%d" % i, [128, 4, 320], BF16) for i in range(NKG_)]
        ptl = [ovl("pt%d" % i, [128, 2, NB], BF16) for i in range(4)]
        zgs = [ovl("zgs%d" % i, [128, NB]) for i in range(4)]
        sqs = [ovl("sqs%d" % i, [128, NB], BF16) for i in range(2)]
        sqs = sqs + sqs
        rcs = [ovl("rc%d" % i, [128, NB], F32, multi=True) for i in range(2)]
        rhl = [ovl("rhl%d" % i, [128, 2, NB], BF16, multi=True) for i in range(2)]
        for i in range(2):
            tk.op("pool", lambda e, i=i: e.memset(rhl[i][0][:], 0.0), writes=[rhl[i][1]])
        ws = wcol_stream(l, [TQ + j for j in range(4)] + [TAG + j for j in range(4)])
        banks = []
        qbanks = [(6, 7), (2, 3), (4, 5), (6, 7)]

        def q_bc(j):
            qk_part_b(zgs[j][0], zgs[j][1], sqs[j][0], sqs[j][1], qbanks[j][0], qbanks[j][1])
            qk_part_c(zgs[j][0], zgs[j][1], qbanks[j][0], qbanks[j][1],
                      [(qrot[0:64, j, :], slice(0, 64)), (qrot[64:128, 4 + j, :], slice(64, 128))], r_qt[j])
        for j in range(4):
            wt, r_wt = ws.get(j)
            bank = nextpj()
            proj(wt, r_wt, bank)
            qk_part_a(bank, 0, zgs[j][0], zgs[j][1], sqs[j][0], sqs[j][1])
            if j >= 1:
                q_bc(j - 1)
        for j in range(4):
            wt, r_wt = ws.get(4 + j)
            bank = nextpj()
            proj(wt, r_wt, bank)
            gate_from(bank, Ga[:, j, :], r_Ga)
            if j == 0:
                q_bc(3)
        k_src, r_ksrc = (kg_d, r_kg) if is_p else (kbuf_d, r_kbuf)
        r_vsrc = r_vg if is_p else r_vbuf

        def v_grp(kgp):
            if is_p:
                return vg_d[kgp % NBLK][(kgp // NBLK) * NB:(kgp // NBLK + 1) * NB, :]
            return vbuf_d[kgp]
        NKG = NKC // 4
        KPQ = T // NB

        def mk_k(kgp):
            def ld(t, r):
                rk, cb = kgp // KPQ, kgp % KPQ
                tk.dma("sp", t[:], k_src[rk * 128:(rk + 1) * 128, cb * NB:(cb + 1) * NB], reads=[r_ksrc], writes=[r])
            return ld

        def mk_v(kgp):
            def ld(t, r):
                tk.dma("sp", t[:], v_grp(kgp).rearrange("(t p) n -> p t n", p=128), reads=[r_vsrc], writes=[r])
            return ld
        sc = float(64 ** -0.5)
        NKP = NKC // 2
        pt_i = [0]
        deferred = []

        def epilogue(h, obank):
            par = h % 2
            hp = h // 2
            orow = slice(0, 64) if par == 0 else slice(64, 128)
            d0 = 64 if par == 0 else 0
            rct, r_rct = rcs[par]
            bb = 6 + par
            tk.op("dve", lambda e: e.reciprocal(out=rct[d0:d0 + 1, :], in_=ps[d0:d0 + 1, obank, :]), reads=[r_ps[obank]], writes=[r_rct])
            selt, r_selt = (selb64, r_selb64) if d0 == 64 else (selb0, r_selb0)
            rh, r_rh = rhl[par]
            tk.op("dve", lambda e: e.tensor_copy(out=rh[d0:d0 + 1, 0, :], in_=rct[d0:d0 + 1, :]), reads=[r_rct], writes=[r_rh])
            tk.op("dve", lambda e: e.tensor_tensor(out=rh[d0:d0 + 1, 1, :], in0=rct[d0:d0 + 1, :], in1=rh[d0:d0 + 1, 0, :], op=ALU.subtract), reads=[r_rct, r_rh], writes=[r_rh])
            mm(ps[:, bb, :], selt[:], rh[:, 0, :], True, False, [r_selt, r_rh], r_ps[bb], sig=False)
            mm(ps[:, bb, :], selt[:], rh[:, 1, :], False, True, [r_selt, r_rh], r_ps[bb])
            t1, r_t1 = nexttmp()
            tk.op("dve", lambda e: e.tensor_tensor(out=t1[orow, :], in0=ps[orow, obank, :], in1=Ga[orow, hp, :], op=ALU.mult),
                  reads=[r_ps[obank], r_Ga], writes=[r_t1])
            tk.op("dve", lambda e: e.tensor_tensor(out=concat[orow, hp, :], in0=t1[orow, :], in1=ps[orow, bb, :], op=ALU.mult),
                  reads=[r_t1, r_ps[bb]], writes=[r_concat])

        slots = [(0, 1), (2, 3), (6, 7)]
        steps = [(h, kp) for h in range(8) for kp in range(NKP)]
        NST = len(steps)
        LOOK = 2
        if is_p:
            s_k = Stream(kst, [mk_k(i % NKG) for i in range(8 * NKG)])
            s_v = Stream(vst, [mk_v(i % NKG) for i in range(8 * NKG)])
        else:
            s_k = Stream(kst, [mk_k(i) for i in range(NKG)])
            s_v = Stream(vst, [mk_v(i) for i in range(NKG)])
        pbuf = {}
        for g in range(NST + LOOK):
            if g < NST:
                h, kp = steps[g]
                s0, s1 = slots[g % 3]
                ktile, r_k = s_k.get((h * NKG if is_p else 0) + kp // 2)
                for i in range(2):
                    kc = 2 * kp + i
                    sb = s0 + i
                    mm(ps[:, sb, :], ktile[:, (kc % 4) * 128:(kc % 4 + 1) * 128], qrot[:, h, :], True, True, [r_k, r_qt[h % 4]], r_ps[sb])
                p_t, r_p = ptl[g % 4]
                tk.op("act", lambda e, p_t=p_t, s0=s0: e.activation(out=p_t[:], in_=ps[:, s0:s0 + 2, :], func=AF.Exp, scale=sc),
                      reads=[r_ps[s0], r_ps[s1]], writes=[r_p])
                pbuf[g] = (p_t, r_p)
            g0 = g - LOOK
            if g0 >= 0:
                h0, kp0 = steps[g0]
                kv0 = h0 // 4
                par0 = h0 % 2
                vcol = (64 if par0 == 0 else 0) + 128 * kv0
                obank = 4 + par0
                p0, r_p0 = pbuf.pop(g0)
                vtile, r_v = s_v.get((h0 * NKG if is_p else 0) + kp0 // 2)
                for i in range(2):
                    kc0 = 2 * kp0 + i
                    mm(ps[:, obank, :], vtile[:, kc0 % 4, vcol:vcol + 128], p0[:, i, :], kc0 == 0, kc0 == NKC - 1, [r_v, r_p0], r_ps[obank],
                       sig=(i == 1))
                if kp0 == NKP - 1:
                    deferred.append((h0, obank))
                if kp0 == 0 and deferred and deferred[0][0] != h0:
                    epilogue(*deferred.pop(0))
        while deferred:
            epilogue(*deferred.pop(0))
        stage()
        vh, r_vh = ovl("vh", [128, 4, 512], BF16, multi=True)
        SG, r_SG = ovl("SG", [128, 4, NB], F32, multi=True)
        svn, r_svn = ovl("svn", [128, 512])
        junk, r_junk = ovl("junk", [128, 128])
        ss4, r_ss4 = ovl("ss4", [128, 4], F32, multi=True)
        rs4, r_rs4 = ovl("rs4", [128, 4])
        wsv, r_wsv = wbig[0]
        tk.dma("sp", wsv[:], wSV_d[l], reads=[r_wscr], writes=[r_wsv])
        for tt in range(4):
            bank = nextpj()
            for c in range(DC):
                mm(ps[:, bank, :], xb[:, c, tt * 128:(tt + 1) * 128], wsv[:, c, :], c == 0, c == DC - 1, [r_xb, r_wsv], r_ps[bank])
            tk.op("dve", lambda e, bank=bank, tt=tt: e.tensor_scalar(out=svn[:], in0=ps[:, bank, :], scalar1=cur["rcol"][:, tt:tt + 1], scalar2=None, op0=ALU.mult),
                  reads=[r_ps[bank], r_rcol], writes=[r_svn])
            for h in range(4):
                tk.op("act", lambda e, h=h: e.activation(out=junk[:], in_=svn[:, h * 128:(h + 1) * 128], func=AF.Square, accum_out=ss4[:, h:h + 1]),
                      reads=[r_svn], writes=[r_junk, r_ss4])
            tk.op("act", lambda e: e.activation(out=rs4[:], in_=ss4[:], func=AF.Ln, bias=epsc[:], scale=1.0 / 128), reads=[r_ss4, r_epsc], writes=[r_rs4])
            tk.op("act", lambda e: e.activation(out=rs4[:], in_=rs4[:], func=AF.Exp, scale=-0.5), reads=[r_rs4], writes=[r_rs4])
            for h in range(4):
                tk.op("dve", lambda e, h=h, tt=tt: e.scalar_tensor_tensor(out=vh[:, tt, h * 128:(h + 1) * 128], in0=svn[:, h * 128:(h + 1) * 128], scalar=rs4[:, h:h + 1],
                                                                       in1=vgb[:, h * 128:(h + 1) * 128], op0=ALU.mult, op1=ALU.mult),
                      reads=[r_svn, r_rs4, r_vgb], writes=[r_vh])
        mqT, r_mqT = ovl("mqT", [128, 4, NB], BF16, multi=True)
        Gm, r_Gm = ovl("Gm", [128, 4, NB], F32, multi=True)
        pm = [ovl("pm%d" % i, [128, NB], BF16) for i in range(4)]
        ws = wcol_stream(l, [TSU + j for j in range(4)] + [TSG + j for j in range(4)])
        for j in range(4):
            wt, r_wt = ws.get(j)
            bank = nextpj()
            proj(wt, r_wt, bank)
            tk.op("dve", lambda e, bank=bank, j=j: e.tensor_tensor(out=SG[:, j, :], in0=ps[:, bank, :], in1=cur["rbc"], op=ALU.mult), reads=[r_ps[bank], r_rbc], writes=[r_SG])
        for j in range(4):
            wt, r_wt = ws.get(4 + j)
            bank = nextpj()
            proj(wt, r_wt, bank)
            t1, r_t1 = nexttmp()
            gate_from(bank, t1[:], r_t1)
            tk.op("pool", lambda e, j=j, t1=t1: e.tensor_tensor(out=SG[:, j, :], in0=SG[:, j, :], in1=t1[:], op=ALU.mult), reads=[r_t1, r_SG], writes=[r_SG])
        ws = wcol_stream(l, [TMQ + j for j in range(4)] + [TMG + j for j in range(4)])
        for j in range(4):
            wt, r_wt = ws.get(j)
            bank = nextpj()
            proj(wt, r_wt, bank)
            tk.op("dve", lambda e, bank=bank, j=j: e.tensor_tensor(out=mqT[:, j, :], in0=ps[:, bank, :], in1=cur["rbc"], op=ALU.mult), reads=[r_ps[bank], r_rbc], writes=[r_mqT])
        for j in range(4):
            wt, r_wt = ws.get(4 + j)
            bank = nextpj()
            proj(wt, r_wt, bank)
            gate_from(bank, Gm[:, j, :], r_Gm)

        def m_qk(h):
            pts = []
            for mc in range(2):
                sb = 2 + mc
                mm(ps[:, sb, :], KmT[:, h, mc * 128:(mc + 1) * 128], mqT[:, h, :], True, True, [r_KmT, r_mqT], r_ps[sb])
                p_t, r_p = pm[(2 * h + mc) % 4]
                tk.op("act", lambda e, p_t=p_t, sb=sb, mc=mc: e.activation(out=p_t[:], in_=ps[:, sb, :], func=AF.Exp, scale=rsm[:, mc:mc + 1]), reads=[r_ps[sb], r_rsm], writes=[r_p])
                pts.append((p_t, r_p))
            return pts

        def sgu_head(h):
            bank = nextpj()
            for tt in range(4):
                mm(ps[:, bank, tt * 128:(tt + 1) * 128], vh[:, tt, h * 128:(h + 1) * 128], wsT[:, h, :], True, True, [r_vh, r_wsT], r_ps[bank], sig=(tt == 3))
            t1, r_t1 = nexttmp()
            tk.op("dve", lambda e, t1=t1: e.tensor_tensor(out=t1[:], in0=ps[:, bank, :], in1=bsb[:, h, :], op=ALU.add), reads=[r_ps[bank], r_bsb], writes=[r_t1])
            tk.op("dve", lambda e, t1=t1: e.tensor_tensor(out=concat[:, 8 + h, :], in0=t1[:], in1=SG[:, h, :], op=ALU.mult), reads=[r_t1, r_SG], writes=[r_concat])

        def m_pv(h, pts):
            ob, db = (4, 5) if h % 2 == 0 else (6, 7)
            for mc in range(2):
                p_t, r_p = pts[mc]
                mm(ps[:, ob, :], Vm[:, mc, h * 128:(h + 1) * 128], p_t[:], mc == 0, mc == 1, [r_Vm, r_p], r_ps[ob])
                mm(ps[:, db, :], onesb[:], p_t[:], mc == 0, mc == 1, [r_onesb, r_p], r_ps[db])
            t1, r_t1 = nexttmp()
            tk.op("act", lambda e, t1=t1: e.activation(out=t1[:], in_=ps[:, db, :], func=AF.Ln), reads=[r_ps[db]], writes=[r_t1])
            tk.op("act", lambda e, t1=t1: e.activation(out=t1[:], in_=t1[:], func=AF.Exp, scale=-1.0), reads=[r_t1], writes=[r_t1])
            tk.op("dve", lambda e, t1=t1: e.tensor_tensor(out=t1[:], in0=t1[:], in1=Gm[:, h, :], op=ALU.mult), reads=[r_t1, r_Gm], writes=[r_t1])
            tk.op("dve", lambda e, t1=t1: e.tensor_tensor(out=concat[:, 12 + h, :], in0=ps[:, ob, :], in1=t1[:], op=ALU.mult), reads=[r_ps[ob], r_t1], writes=[r_concat])

        for h in range(4):
            pts = m_qk(h)
            sgu_head(h)
            m_pv(h, pts)
        stage()
        oT, r_oT = ovl("oT", [128, DC, NB], F32, multi=True)
        wo = [ovl("wo%d" % i, [128, 16, 128], BF16) for i in range(3)]

        def mk_wo(dt):
            def ld(t, r):
                tk.dma("sp", t[:], wO_d[l, dt], reads=[r_wscr], writes=[r])
            return ld
        s_wo = Stream(wo, [mk_wo(dt) for dt in range(8)])
        pend_ss = None
        for dt in range(8):
            wt, r_wt = s_wo.get(dt)
            bank = nextpj()
            for ct in range(16):
                mm(ps[:, bank, :], wt[:, ct, :], concat[:, ct, :], ct == 0, ct == 15, [r_wt, r_concat], r_ps[bank])
            tk.op("act", lambda e, bank=bank, dt=dt: e.copy(out=oT[:, dt, :], in_=ps[:, bank, :]), reads=[r_ps[bank]], writes=[r_oT])
            sq, r_sq = nextsq()
            tk.op("dve", lambda e, sq=sq, dt=dt: e.tensor_tensor(out=sq[:], in0=oT[:, dt, :], in1=oT[:, dt, :], op=ALU.mult), reads=[r_oT], writes=[r_sq])
            if pend_ss is not None:
                mm(ps[:, 6, :], onesb[:], pend_ss[0][:], pend_ss[2] == 0, False, [r_onesb, pend_ss[1]], r_ps[6], sig=True)
            pend_ss = (sq, r_sq, dt)
        mm(ps[:, 6, :], onesb[:], pend_ss[0][:], False, True, [r_onesb, pend_ss[1]], r_ps[6])
        if b + 1 < NBLK:
            make_xb(b + 1)
        t1, r_t1 = nexttmp()
        tk.op("act", lambda e: e.activation(out=t1[:], in_=ps[:, 6, :], func=AF.Ln, bias=epsc[:], scale=1.0 / D), reads=[r_ps[6], r_epsc], writes=[r_t1])
        tk.op("act", lambda e: e.activation(out=t1[:], in_=t1[:], func=AF.Exp, scale=-0.5), reads=[r_t1], writes=[r_t1])
        for dt in range(8):
            eng = "dve" if dt % 2 == 0 else "pool"
            tk.op("dve", lambda e, dt=dt: e.scalar_tensor_tensor(out=oT[:, dt, :], in0=oT[:, dt, :], scalar=gcol[:, 8 + dt:9 + dt], in1=t1[:], op0=ALU.mult, op1=ALU.mult),
                  reads=[r_oT, r_gcol, r_t1], writes=[r_oT])
            tk.op(eng, lambda e, dt=dt: e.tensor_tensor(out=xT[:, dt, b * NB:(b + 1) * NB], in0=xT[:, dt, b * NB:(b + 1) * NB], in1=oT[:, dt, :], op=ALU.add),
                  reads=[r_oT, r_xTb[b]], writes=[r_xTb[b]])

    def load_chunk(ch):
        stage()
        tok = [ovl("tok%d" % i, [128, D]) for i in range(2)]
        for tt in range(NTT):
            t_t, r_t = tok[tt % 2]
            tk.dma("sp", t_t[:], x_d[ch, tt * 128:(tt + 1) * 128, :], writes=[r_t])
            for half in range(2):
                bank = nextpj()
                for c4 in range(4):
                    c = half * 4 + c4
                    tr(ps[:, bank, c4 * 128:(c4 + 1) * 128], t_t[:, c * 128:(c + 1) * 128], [r_t], r_ps[bank], sig=(c4 == 3))
                eng = "dve" if half == 0 else "act"
                if eng == "dve":
                    tk.op("dve", lambda e, bank=bank, half=half, tt=tt: e.tensor_copy(out=xT[:, half * 4:half * 4 + 4, tt * 128:(tt + 1) * 128],
                                                                                     in_=ps[:, bank, :].rearrange("p (c n) -> p c n", c=4)),
                          reads=[r_ps[bank]], writes=[r_xTb[tt // 4]])
                else:
                    tk.op("act", lambda e, bank=bank, half=half, tt=tt: e.copy(out=xT[:, half * 4:half * 4 + 4, tt * 128:(tt + 1) * 128],
                                                                               in_=ps[:, bank, :].rearrange("p (c n) -> p c n", c=4)),
                          reads=[r_ps[bank]], writes=[r_xTb[tt // 4]])

    def store_chunk(ch):
        stage()
        tok = [ovl("otok%d" % i, [128, D], F32, multi=True) for i in range(2)]
        for tt in range(NTT):
            t_t, r_t = tok[tt % 2]
            for half in range(2):
                bank = nextpj()
                for c4 in range(4):
                    c = half * 4 + c4
                    tr(ps[:, bank, c4 * 128:(c4 + 1) * 128], xT[:, c, tt * 128:(tt + 1) * 128], [r_xTb[tt // 4]], r_ps[bank], sig=(c4 == 3))
                if half == 0:
                    tk.op("dve", lambda e, bank=bank, t_t=t_t: e.tensor_copy(out=t_t[:, 0:512], in_=ps[:, bank, :]), reads=[r_ps[bank]], writes=[r_t])
                else:
                    tk.op("act", lambda e, bank=bank, t_t=t_t: e.copy(out=t_t[:, 512:1024], in_=ps[:, bank, :]), reads=[r_ps[bank]], writes=[r_t])
            tk.dma("pool", y_d[ch, tt * 128:(tt + 1) * 128, :], t_t[:], reads=[r_t], writes=[r_y])

    for ch in range(NCH):
        is_p = has_p and ch == 0
        pt = 0 if is_p else 1
        load_chunk(ch)
        for l in range(L):
            load_layer_consts(l)
            phase1(l, pt)
            if is_p:
                gather_prompt()
            mem_prep(ch, l)
            phase2(l, is_p)
            for b in range(NBLK):
                phase3_block(l, b, pt, is_p)
        store_chunk(ch)
    tk.wait_all("sp", [r_y])
    tk.wait_all("pool", [r_y])
    return nc, tk, st


def _rope_tables(pos0, T):
    t = np.arange(pos0, pos0 + T)
    row = (t // 64).astype(np.float32)
    col = (t % 64).astype(np.float32)
    inv = (np.float32(10000.0) ** (-np.arange(16, dtype=np.float32) / np.float32(16))).astype(np.float32)
    ang_r = row[:, None] * inv[None, :]
    ang_c = col[:, None] * inv[None, :]
    out = np.zeros((2, 128, T), np.float32)
    for p in range(128):
        d = p % 64
        half, j = d // 32, d % 32
        i = j % 16
        a = (ang_r if half == 0 else ang_c)[:, i]
        out[0, p] = np.cos(a)
        out[1, p] = np.sin(a)
    return out


def _consts():
    c = np.zeros((5, 128, 128), np.float32)
    c[0] = np.eye(128, dtype=np.float32)
    R = np.zeros((128, 128), np.float32)
    for p in range(128):
        d = p % 64
        j = d % 32
        if j < 16:
            R[p, p + 16] = -1.0
        else:
            R[p, p - 16] = 1.0
    c[1] = R.T
    c[2] = np.kron(np.eye(2, dtype=np.float32), np.ones((64, 64), np.float32))
    k = np.arange(128)
    ang = 2.0 * np.pi * ((k[:, None] * k[None, :]) % 128) / 128.0
    c[3] = (np.cos(ang) / np.sqrt(128.0)).astype(np.float32)
    c[4] = (np.sin(ang) / np.sqrt(128.0)).astype(np.float32)
    return c


def _dft(S, k0, nk):
    t = np.arange(S, dtype=np.int64)[:, None]
    k = np.arange(k0, k0 + nk, dtype=np.int64)[None, :]
    ang = 2.0 * np.pi * ((t * k) % S).astype(np.float64) / S
    out = np.empty((2, S, nk), ml_dtypes.bfloat16)
    out[0] = (np.cos(ang) / np.sqrt(S)).astype(np.float32).astype(ml_dtypes.bfloat16)
    out[1] = (-np.sin(ang) / np.sqrt(S)).astype(np.float32).astype(ml_dtypes.bfloat16)
    return out


def _host_layout(cfg, x_chunks, mem_chunks, pos0s, w, core):
    L = cfg["depth"]
    T = cfg["T"]
    g = np.zeros((L, 128, 32), np.float32)
    for l in range(L):
        g[l, :, 0:8] = w["pre_norm_g"][l].reshape(8, 128).T
        g[l, :, 8:16] = w["post_norm_g"][l].reshape(8, 128).T
        g[l, :, 16:24] = w["mem_norm_g"][l].reshape(8, 128).T
        g[l, :, 24] = np.tile(w["q_norm_g"][l], 2)
        g[l, :, 25] = np.tile(w["k_norm_g"][l], 2)
        g[l, :, 26:] = 1.0
    vgb = np.ascontiguousarray(np.broadcast_to(w["sgu_norm_g"].reshape(L, 1, 512), (L, 128, 512)))
    bsb = np.ascontiguousarray(np.broadcast_to(np.tile(w["b_spatial"].reshape(L, 1, 4, 128), (1, 1, 1, NB // 128)), (L, 128, 4, NB)))
    m = {
        "x": np.ascontiguousarray(x_chunks), "mem": np.ascontiguousarray(mem_chunks),
        "w_in": w["w_in"], "w_out": w["w_out"], "w_mem_kv": w["w_mem_kv"], "w_fourier": w["w_fourier"],
        "w_spatial": w["w_spatial"], "gcols": g, "vgb": vgb, "bsb": bsb, "cst": _consts(),
        "rope": np.stack([_rope_tables(pos0s[0], T), _rope_tables(pos0s[1], T)]),
        "dft_s": _DFT_CACHE.setdefault(("s", T), _dft(T, 0, T)),
    }
    if cfg["prompt"]:
        q = core % 4
        m["dft_p"] = _DFT_CACHE.setdefault(("p", T, q), _dft(4 * T, q * T, T))
    return m


_DFT_CACHE = {}
_NC_CACHE = {}


def kernel(x_prompt, x_sample, mem_prompt, mem_sample, pre_norm_g, w_in, q_norm_g, k_norm_g, w_fourier,
           sgu_norm_g, w_spatial, b_spatial, mem_norm_g, w_mem_kv, w_out, post_norm_g):
    f = lambda a: np.ascontiguousarray(np.asarray(a, dtype=np.float32))
    w = dict(pre_norm_g=f(pre_norm_g), w_in=f(w_in), q_norm_g=f(q_norm_g), k_norm_g=f(k_norm_g), w_fourier=f(w_fourier),
             sgu_norm_g=f(sgu_norm_g), w_spatial=f(w_spatial), b_spatial=f(b_spatial), mem_norm_g=f(mem_norm_g),
             w_mem_kv=f(w_mem_kv), w_out=f(w_out), post_norm_g=f(post_norm_g))
    x_prompt, x_sample, mem_prompt, mem_sample = f(x_prompt), f(x_sample), f(mem_prompt), f(mem_sample)
    cfg = dict(depth=2, T=2048, n_samp=4, prompt=True)
    T = cfg["T"]
    if "nc" not in _NC_CACHE:
        _NC_CACHE["nc"] = build(cfg)[0]
    nc = _NC_CACHE["nc"]
    in_maps = []
    for c in range(8):
        pb, q = c // 4, c % 4
        xs = np.concatenate([x_prompt[pb, q * T:(q + 1) * T][None], x_sample[4 * c:4 * c + 4]], axis=0)
        ms = np.concatenate([mem_prompt[pb][None], mem_sample[4 * c:4 * c + 4]], axis=0)
        in_maps.append(_host_layout(cfg, xs, ms, (q * T, 0), w, c))
    res = run_bass_kernel_spmd(nc, in_maps, core_ids=list(range(8)))
    y_prompt = np.empty_like(x_prompt)
    y_sample = np.empty_like(x_sample)
    for c in range(8):
        y = res.results[c]["y"]
        pb, q = c // 4, c % 4
        y_prompt[pb, q * T:(q + 1) * T] = y[0]
        y_sample[4 * c:4 * c + 4] = y[1:]
    return (y_prompt, y_sample)
```

```python
import numpy as np
import ml_dtypes
import concourse.bass as bass
import concourse.mybir as mybir
from concourse.bass_utils import run_bass_kernel_spmd

F32 = mybir.dt.float32
BF16 = mybir.dt.bfloat16
ALU = mybir.AluOpType
AF = mybir.ActivationFunctionType

D = 1024
DC = 8
INW = 4864
NB = 512
EPS = 1e-6
NMEM = 256
SAME_ENG_SYNC = True

O_AQ, O_AK, O_AV, O_AG, O_FA, O_FG, O_SU, O_SV, O_SG, O_MQ, O_MG = (
    0, 512, 640, 768, 1280, 1792, 2304, 2816, 3328, 3840, 4352)
TQ, TK, TAG, TFA, TFG, TSU, TSG, TMQ, TMG, TV = 0, 4, 5, 9, 13, 17, 21, 25, 29, 33
NTA = 34


def _tile_cols(i):
    if i < 4:
        return [(O_AQ + 64 * i, 64), (O_AQ + 256 + 64 * i, 64)]
    if i == TK:
        return [(O_AK, 128)]
    if i == TV:
        return [(O_AV, 128)]
    for base, off in ((TAG, O_AG), (TFA, O_FA), (TFG, O_FG), (TSU, O_SU), (TSG, O_SG), (TMQ, O_MQ), (TMG, O_MG)):
        if base <= i < base + 4:
            return [(off + 128 * (i - base), 128)]
    raise ValueError(i)


class Res:
    __slots__ = ("name", "ws", "rs", "prev", "multi")

    def __init__(self, name, multi=False):
        self.name = name
        self.ws = {}
        self.rs = {}
        self.prev = {}
        self.multi = multi


def _merge(d, toks):
    for s, v in toks.items():
        if d.get(s, 0) < v:
            d[s] = v


class Trk:
    def __init__(self, nc):
        self.nc = nc
        self.eng = {"pe": nc.tensor, "act": nc.scalar, "dve": nc.vector, "pool": nc.gpsimd, "sp": nc.sync}
        self.sem = {k: nc.alloc_semaphore("es_" + k) for k in self.eng}
        self.cnt = {k: 0 for k in self.eng}
        self.waited = {k: {} for k in self.eng}
        self.ring = {}
        self.ring_i = {}
        for q in ("sp", "pool", "act"):
            self.ring[q] = [[nc.alloc_semaphore("dq_%s_%d" % (q, i)), 0] for i in range(12)]
            self.ring_i[q] = 0
        self.n_inst = 0

    def _deps(self, reads, writes):
        toks = {}
        for r in reads:
            _merge(toks, r.ws)
        for w in writes:
            if w.rs:
                _merge(toks, w.rs)
                _merge(toks, w.ws)
            else:
                _merge(toks, w.prev)
                if not w.multi:
                    _merge(toks, w.ws)
        return toks

    def _commit(self, reads, writes, tok):
        for w in writes:
            if w.rs:
                w.prev = {}
                _merge(w.prev, w.rs)
                _merge(w.prev, w.ws)
                w.ws = {}
                w.rs = {}
            if not w.multi:
                w.ws = {}
            _merge(w.ws, tok)
        for r in reads:
            if r not in writes:
                _merge(r.rs, tok)

    def _wait(self, en, toks):
        e = self.eng[en]
        wd = self.waited[en]
        for s, v in toks.items():
            if s is self.sem[en] and (en == "pe" or not SAME_ENG_SYNC):
                continue
            if wd.get(s, 0) >= v:
                continue
            e.wait_ge(s, v)
            wd[s] = v
            self.n_inst += 1

    def op(self, en, fn, reads=(), writes=(), sig=True):
        toks = self._deps(reads, writes)
        self._wait(en, toks)
        inst = fn(self.eng[en])
        self.n_inst += 1
        if sig:
            self.cnt[en] += 1
            inst.then_inc(self.sem[en], 1)
            tok = {self.sem[en]: self.cnt[en]}
        else:
            tok = {self.sem[en]: self.cnt[en] + 1}
        self._commit(reads, writes, tok)
        return inst

    def dma(self, q, out, in_, reads=(), writes=()):
        toks = self._deps(reads, writes)
        slot = self.ring[q][self.ring_i[q] % len(self.ring[q])]
        self.ring_i[q] += 1
        if slot[1]:
            toks = dict(toks)
            _merge(toks, {slot[0]: slot[1]})
        self._wait(q, toks)
        inst = self.eng[q].dma_start(out=out, in_=in_)
        self.n_inst += 1
        slot[1] += 16
        inst.then_inc(slot[0], 16)
        tok = {slot[0]: slot[1]}
        self._commit(reads, writes, tok)
        return tok

    def wait_all(self, en, ress):
        toks = {}
        for r in ress:
            _merge(toks, r.ws)
            _merge(toks, r.rs)
        self._wait(en, toks)


class Stream:
    def __init__(self, slots, loaders, depth=None):
        self.slots = slots
        self.loaders = loaders
        self.issued = 0
        self.depth = depth or len(slots)

    def get(self, i):
        while self.issued < min(len(self.loaders), i + self.depth):
            j = self.issued
            t, r = self.slots[j % len(self.slots)]
            self.loaders[j](t, r)
            self.issued += 1
        return self.slots[i % len(self.slots)]


def build(cfg):
    L = cfg["depth"]
    T = cfg["T"]
    NBLK = T // NB
    NTT = T // 128
    has_p = cfg["prompt"]
    NS = cfg["n_samp"]
    NCH = NS + (1 if has_p else 0)
    SP = 4 * T
    nc = bass.Bass("TRN2", target_bir_lowering=False)
    tk = Trk(nc)

    def din(name, shape, dt=F32):
        return nc.dram_tensor(name, list(shape), dt, kind="ExternalInput").ap()

    x_d = din("x", [NCH, T, D])
    mem_d = din("mem", [NCH, NMEM, D])
    w_in_d = din("w_in", [L, D, INW])
    w_out_d = din("w_out", [L, 2048, D])
    w_mkv_d = din("w_mem_kv", [L, D, 1024])
    w_four_d = din("w_fourier", [L, 4, 128, 128])
    w_spat_d = din("w_spatial", [L, 4, 128, 128])
    gcols_d = din("gcols", [L, 128, 32])
    vgb_d = din("vgb", [L, 128, 512])
    bsb_d = din("bsb", [L, 128, 4, NB])
    cst_d = din("cst", [5, 128, 128])
    rope_d = din("rope", [2, 2, 128, T])
    dfts_d = din("dft_s", [2, T, T], BF16)
    if has_p:
        dftp_d = din("dft_p", [2, SP, T], BF16)
    y_d = nc.dram_tensor("y", [NCH, T, D], F32, kind="ExternalOutput").ap()

    def dscr(name, shape, dt=BF16):
        return nc.dram_tensor(name, list(shape), dt).ap()

    wA_d = dscr("wA", [L, NTA, 128, DC, 128])
    wSV_d = dscr("wSV", [L, 128, DC, 512])
    wO_d = dscr("wO", [L, 8, 128, 16, 128])
    wMK_d = dscr("wMK", [L, 4, 128, DC, 128])
    wMV_d = dscr("wMV", [L, 128, DC, 512])
    wcws_d = dscr("wcws", [L, 128, 4, 256])
    wsT_d = dscr("wsT", [L, 128, 4, 128])
    kbuf_d = dscr("kbuf", [128, T])
    vbuf_d = [dscr("vbuf%d" % i, [NB, 320]) for i in range(NBLK)]
    uvbuf_d = [dscr("uvbuf%d" % i, [NB, 1024]) for i in range(NBLK)]
    cf_d = dscr("cfbuf", [128, 4, T])
    if has_p:
        kg_d = dscr("kg", [4 * 128, T])
        vg_d = [dscr("vg%d" % i, [4 * NB, 320]) for i in range(NBLK)]
        uvg_d = [dscr("uvg%d" % i, [4 * NB, 1024]) for i in range(NBLK)]
    r_wscr = Res("wscr", multi=True)
    r_kbuf, r_vbuf, r_uvbuf, r_cf = Res("kbuf", True), Res("vbuf", True), Res("uvbuf", True), Res("cf", True)
    r_kg, r_vg, r_uvg = Res("kg", True), Res("vg", True), Res("uvg", True)
    r_y = Res("y", True)

    SB_LO = 16512
    SB_HI = 229344
    st = {"p": SB_LO, "ov0": None, "ov": None, "ovmax": 0, "live": []}
    cnt = [0]

    def _bytes(shape, dt):
        n = 1
        for s in shape[1:]:
            n *= s
        return n * (2 if dt == BF16 else 4)

    def _alloc(name, shape, dt, off):
        cnt[0] += 1
        return nc.alloc_sbuf_tensor_at("%s_%d" % (name, cnt[0]), list(shape), dt, offset=off)

    def persist(name, shape, dt=F32, multi=False):
        assert st["ov0"] is None
        b = (_bytes(shape, dt) + 31) // 32 * 32
        t = _alloc(name, shape, dt, st["p"])
        st["p"] += b
        assert st["p"] <= SB_HI, "sbuf overflow persist %s" % name
        return t, Res(name, multi)

    def ov_begin():
        st["ov0"] = st["p"]
        st["ov"] = st["p"]

    def stage():
        inh = {}
        for r in st["live"]:
            _merge(inh, r.ws)
            _merge(inh, r.rs)
            _merge(inh, r.prev)
        _merge(inh, st.get("inh", {}))
        st["inh"] = inh
        st["live"] = []
        st["ov"] = st["ov0"]

    def ovl(name, shape, dt=F32, multi=False):
        b = (_bytes(shape, dt) + 31) // 32 * 32
        t = _alloc(name, shape, dt, st["ov"])
        st["ov"] += b
        st["ovmax"] = max(st["ovmax"], st["ov"])
        assert st["ov"] <= SB_HI, "sbuf overflow overlay %s (%d)" % (name, st["ov"])
        r = Res(name, multi)
        r.prev = dict(st.get("inh", {}))
        st["live"].append(r)
        return t, r

    xT, r_xT_all = persist("xT", [128, DC, T], F32, multi=True)
    r_xTb = [Res("xT_b%d" % i, multi=True) for i in range(NBLK)]
    xb, r_xb = persist("xb", [128, DC, NB], BF16, multi=True)
    rbc_all, r_rbc = persist("rbc_all", [128, NBLK, NB], F32, multi=True)
    rcol_all, r_rcol = persist("rcol_all", [128, NBLK * 4], F32, multi=True)
    cur = {}
    cosb, r_cos = persist("cosb", [128, NB])
    sinb, r_sin = persist("sinb", [128, NB])
    wcol = [persist("wcol%d" % i, [128, DC, 128], BF16) for i in range(4)]
    wbig = [persist("wbig%d" % i, [128, DC, 512], BF16) for i in range(1)]
    tmp = [persist("tmp%d" % i, [128, NB]) for i in range(6)]
    sqb = [persist("sqb%d" % i, [128, NB], BF16) for i in range(2)]
    concat, r_concat = persist("concat", [128, 16, NB], BF16, multi=True)
    KmT, r_KmT = persist("KmT", [128, 4, NMEM], BF16, multi=True)
    Vm, r_Vm = persist("Vm", [128, 2, 512], BF16, multi=True)
    rsm, r_rsm = persist("rsm", [128, 2])
    ident, r_ident = persist("ident", [128, 128])
    rotT, r_rotT = persist("rotT", [128, 128])
    bones, r_bones = persist("bones", [128, 128], BF16)
    onesb, r_onesb = persist("onesb", [128, 128], BF16)
    onesf, r_onesf = persist("onesf", [128, 128])
    c128, r_c128 = persist("c128", [128, 128])
    s128, r_s128 = persist("s128", [128, 128])
    epsc, r_epsc = persist("epsc", [128, 1])
    sel0, r_sel0 = persist("sel0", [128, 128])
    sel64, r_sel64 = persist("sel64", [128, 128])
    wcws, r_wcws = persist("wcws", [128, 4, 256], BF16)
    wsT, r_wsT = persist("wsT", [128, 4, 128], BF16)
    bsb, r_bsb = persist("bsb", [128, 4, NB])
    vgb, r_vgb = persist("vgb", [128, 512])
    gcol, r_gcol = persist("gcol", [128, 32])
    ginv, r_ginv = persist("ginv", [128, 2])
    tmp_i = [0]
    ov_begin()

    ps = nc.alloc_psum_tensor("ps", [128, 8, NB], F32)
    r_ps = [Res("ps%d" % i, multi=True) for i in range(8)]

    def nexttmp():
        tmp_i[0] += 1
        return tmp[tmp_i[0] % len(tmp)]

    sq_i = [0]

    def nextsq():
        sq_i[0] += 1
        return sqb[sq_i[0] % 2]

    pj_i = [0]

    def nextpj():
        pj_i[0] += 1
        return pj_i[0] % 2

    def mm(out, lhsT, rhs, start, stop, reads, w, sig=None):
        tk.op("pe", lambda e: e.matmul(out, lhsT=lhsT, rhs=rhs, start=start, stop=stop),
              reads=reads, writes=[w], sig=stop if sig is None else sig)

    def tr(out, in_, reads, w, sig=True):
        tk.op("pe", lambda e: e.transpose(out, in_, ident[:]), reads=list(reads) + [r_ident], writes=[w], sig=sig)

    stage()
    cst_f, r_cstf = ovl("cst_f", [128, 5, 128])
    tk.dma("sp", cst_f[:], cst_d.rearrange("k p n -> p k n"), writes=[r_cstf])
    tk.op("dve", lambda e: e.tensor_copy(out=ident[:], in_=cst_f[:, 0, :]), reads=[r_cstf], writes=[r_ident])
    tk.op("dve", lambda e: e.tensor_copy(out=rotT[:], in_=cst_f[:, 1, :]), reads=[r_cstf], writes=[r_rotT])
    tk.op("dve", lambda e: e.tensor_copy(out=bones[:], in_=cst_f[:, 2, :]), reads=[r_cstf], writes=[r_bones])
    tk.op("dve", lambda e: e.tensor_copy(out=c128[:], in_=cst_f[:, 3, :]), reads=[r_cstf], writes=[r_c128])
    tk.op("dve", lambda e: e.tensor_copy(out=s128[:], in_=cst_f[:, 4, :]), reads=[r_cstf], writes=[r_s128])
    tk.op("pool", lambda e: e.memset(onesb[:], 1.0), writes=[r_onesb])
    tk.op("pool", lambda e: e.memset(onesf[:], 1.0), writes=[r_onesf])
    tk.op("pool", lambda e: e.memset(epsc[:], EPS), writes=[r_epsc])
    tk.op("pool", lambda e: e.memset(sel0[:], 0.0), writes=[r_sel0])
    tk.op("pool", lambda e: e.memset(sel0[0:1, :], 1.0), writes=[r_sel0])
    tk.op("pool", lambda e: e.memset(sel64[:], 0.0), writes=[r_sel64])
    tk.op("pool", lambda e: e.memset(sel64[64:65, :], 1.0), writes=[r_sel64])

    def prep_layer(l):
        stage()
        gexp, r_gexp = ovl("gexp", [128, DC, 128])
        gmexp, r_gmexp = ovl("gmexp", [128, DC, 128])
        stg = [ovl("stg%d" % i, [128, DC, 128]) for i in range(2)]
        stb = [ovl("stb%d" % i, [128, DC, 128], BF16) for i in range(2)]
        sto = [ovl("sto%d" % i, [128, 16, 128]) for i in range(2)]
        stob = [ovl("stob%d" % i, [128, 16, 128], BF16) for i in range(2)]
        wf, r_wf = ovl("wf", [128, 4, 128])
        wsp, r_wsp = ovl("wsp", [128, 4, 128])
        wcws_s, r_wcws_s = ovl("wcws_s", [128, 4, 256], BF16, multi=True)
        wsT_s, r_wsT_s = ovl("wsT_s", [128, 4, 128], BF16, multi=True)
        tk.dma("sp", gcol[:], gcols_d[l], writes=[r_gcol])
        for c in range(DC):
            tk.op("dve", lambda e, c=c: e.tensor_scalar(out=gexp[:, c, :], in0=onesf[:], scalar1=gcol[:, c:c + 1], scalar2=None, op0=ALU.mult),
                  reads=[r_onesf, r_gcol], writes=[r_gexp] if c == 0 else [r_gexp])
            tk.op("dve", lambda e, c=c: e.tensor_scalar(out=gmexp[:, c, :], in0=onesf[:], scalar1=gcol[:, 16 + c:17 + c], scalar2=None, op0=ALU.mult),
                  reads=[r_onesf, r_gcol], writes=[r_gmexp])
        r_gexp.multi = True
        r_gmexp.multi = True
        n = [0]

        def conv(src_aps, dst_ap, gx, r_gx):
            i = n[0] % 2
            n[0] += 1
            (sg, r_sg), (sbb, r_sbb) = stg[i], stb[i]
            r_sg.multi = True
            off = 0
            for a, w_ in src_aps:
                tk.dma("sp", sg[:, :, off:off + w_], a, writes=[r_sg])
                off += w_
            if gx is None:
                tk.op("dve", lambda e: e.tensor_copy(out=sbb[:], in_=sg[:]), reads=[r_sg], writes=[r_sbb])
            else:
                tk.op("dve", lambda e: e.tensor_tensor(out=sbb[:], in0=sg[:], in1=gx[:], op=ALU.mult), reads=[r_sg, r_gx], writes=[r_sbb])
            tk.dma("pool", dst_ap, sbb[:], reads=[r_sbb], writes=[r_wscr])

        win_v = w_in_d[l].rearrange("(c p) n -> p c n", p=128)
        for i in range(NTA):
            conv([(win_v[:, :, a:a + w_], w_) for a, w_ in _tile_cols(i)], wA_d[l, i], gexp, r_gexp)
        for j in range(4):
            conv([(win_v[:, :, O_SV + 128 * j:O_SV + 128 * (j + 1)], 128)], wSV_d[l][:, :, 128 * j:128 * (j + 1)], gexp, r_gexp)
        wm_v = w_mkv_d[l].rearrange("(c p) n -> p c n", p=128)
        for h in range(4):
            conv([(wm_v[:, :, 128 * h:128 * (h + 1)], 128)], wMK_d[l, h], gmexp, r_gmexp)
        for j in range(4):
            conv([(wm_v[:, :, 512 + 128 * j:512 + 128 * (j + 1)], 128)], wMV_d[l][:, :, 128 * j:128 * (j + 1)], gmexp, r_gmexp)
        wo_v = w_out_d[l].rearrange("(c p) n -> p c n", p=128)
        for dt in range(8):
            (so, r_so), (sob, r_sob) = sto[dt % 2], stob[dt % 2]
            tk.dma("sp", so[:], wo_v[:, :, 128 * dt:128 * (dt + 1)], writes=[r_so])
            tk.op("dve", lambda e: e.tensor_copy(out=sob[:], in_=so[:]), reads=[r_so], writes=[r_sob])
            tk.dma("pool", wO_d[l, dt], sob[:], reads=[r_sob], writes=[r_wscr])
        tk.dma("sp", wf[:], w_four_d[l].rearrange("g j c -> j g c"), writes=[r_wf])
        tk.dma("sp", wsp[:], w_spat_d[l].rearrange("h p q -> p h q"), writes=[r_wsp])
        for g in range(4):
            mm(ps[:, 6, 0:128], c128[:], wf[:, g, :], True, True, [r_c128, r_wf], r_ps[6])
            mm(ps[:, 6, 128:256], s128[:], wf[:, g, :], True, True, [r_s128, r_wf], r_ps[6])
            tk.op("dve", lambda e, g=g: e.tensor_copy(out=wcws_s[:, g, :], in_=ps[:, 6, 0:256]), reads=[r_ps[6]], writes=[r_wcws_s])
            tr(ps[:, 7, 0:128], wsp[:, g, :], [r_wsp], r_ps[7])
            tk.op("dve", lambda e, g=g: e.tensor_copy(out=wsT_s[:, g, :], in_=ps[:, 7, 0:128]), reads=[r_ps[7]], writes=[r_wsT_s])
        tk.dma("pool", wcws_d[l], wcws_s[:], reads=[r_wcws_s], writes=[r_wscr])
        tk.dma("pool", wsT_d[l], wsT_s[:], reads=[r_wsT_s], writes=[r_wscr])

    for l in range(L):
        prep_layer(l)

    def load_layer_consts(l):
        tk.dma("sp", gcol[:], gcols_d[l], writes=[r_gcol])
        tk.dma("sp", wcws[:], wcws_d[l], reads=[r_wscr], writes=[r_wcws])
        tk.dma("sp", wsT[:], wsT_d[l], reads=[r_wscr], writes=[r_wsT])
        tk.dma("sp", bsb[:], bsb_d[l], writes=[r_bsb])
        tk.dma("sp", vgb[:], vgb_d[l], writes=[r_vgb])
        tk.op("dve", lambda e: e.reciprocal(out=ginv[:], in_=gcol[:, 24:26]), reads=[r_gcol], writes=[r_ginv])

    P3_IDS = ([TQ + j for j in range(4)] + [TAG + j for j in range(4)] + [TSU + j for j in range(4)] + [TSG + j for j in range(4)]
              + [TMQ + j for j in range(4)] + [TMG + j for j in range(4)])
    G_IDS = []
    for _ch in range(NCH):
        for _l in range(L):
            for _b in range(NBLK):
                G_IDS += [(_l, i) for i in [TK, TV] + [TFA + j for j in range(4)]]
            for _b in range(NBLK):
                G_IDS += [(_l, TFG + j) for j in range(4)]
            for _b in range(NBLK):
                G_IDS += [(_l, i) for i in P3_IDS]

    def _mk_w(li):
        def ld(t, r):
            tk.dma("sp", t[:], wA_d[li[0], li[1]], reads=[r_wscr], writes=[r])
        return ld
    g_ws = Stream(wcol, [_mk_w(li) for li in G_IDS])
    g_pos = [0]

    class _WS:
        def __init__(self, l, ids):
            self.base = g_pos[0]
            for k, i in enumerate(ids):
                assert G_IDS[self.base + k] == (l, i), (self.base, k, G_IDS[self.base + k], (l, i))
            g_pos[0] += len(ids)

        def get(self, j):
            return g_ws.get(self.base + j)

    def wcol_stream(l, ids):
        return _WS(l, ids)

    def make_xb(b, stats=False):
        cur["rbc"] = rbc_all[:, b, :]
        cur["rcol"] = rcol_all[:, 4 * b:4 * b + 4]
        for c in range(DC):
            src = xT[:, c, b * NB:(b + 1) * NB]
            if c % 2 == 0:
                tk.op("dve", lambda e, c=c, src=src: e.tensor_copy(out=xb[:, c, :], in_=src), reads=[r_xTb[b]], writes=[r_xb])
            else:
                tk.op("act", lambda e, c=c, src=src: e.copy(out=xb[:, c, :], in_=src), reads=[r_xTb[b]], writes=[r_xb])
        if not stats:
            return
        for c in range(DC):
            sq, r_sq = nextsq()
            src = xT[:, c, b * NB:(b + 1) * NB]
            tk.op("act", lambda e, sq=sq, src=src: e.activation(out=sq[:], in_=src, func=AF.Square), reads=[r_xTb[b]], writes=[r_sq])
            mm(ps[:, 6, :], onesb[:], sq[:], c == 0, c == DC - 1, [r_onesb, r_sq], r_ps[6], sig=False)
            for tt in range(4):
                mm(ps[:, 7, tt:tt + 1], sq[:, tt * 128:(tt + 1) * 128], onesb[:, 0:1], c == 0, c == DC - 1, [r_onesb, r_sq], r_ps[7],
                   sig=(tt == 3))
        t1, r_t1 = nexttmp()
        tk.op("act", lambda e: e.activation(out=t1[:], in_=ps[:, 6, :], func=AF.Ln, bias=epsc[:], scale=1.0 / D), reads=[r_ps[6], r_epsc], writes=[r_t1])
        tk.op("act", lambda e: e.activation(out=cur["rbc"], in_=t1[:], func=AF.Exp, scale=-0.5), reads=[r_t1], writes=[r_rbc])
        t2, r_t2 = nexttmp()
        tk.op("act", lambda e: e.activation(out=t2[:, 0:4], in_=ps[:, 7, 0:4], func=AF.Ln, bias=epsc[:], scale=1.0 / D), reads=[r_ps[7], r_epsc], writes=[r_t2])
        tk.op("act", lambda e: e.activation(out=cur["rcol"], in_=t2[:, 0:4], func=AF.Exp, scale=-0.5), reads=[r_t2], writes=[r_rcol])

    def proj(wt, r_wt, bank):
        for c in range(DC):
            mm(ps[:, bank, :], wt[:, c, :], xb[:, c, :], c == 0, c == DC - 1, [r_wt, r_xb], r_ps[bank])

    def load_rope(pt, b):
        tk.dma("sp", cosb[:], rope_d[pt, 0][:, b * NB:(b + 1) * NB], writes=[r_cos])
        tk.dma("sp", sinb[:], rope_d[pt, 1][:, b * NB:(b + 1) * NB], writes=[r_sin])

    def qk_part_a(bank, gi, zg, r_zg, sq, r_sq):
        tk.op("dve", lambda e: e.scalar_tensor_tensor(out=zg[:], in0=ps[:, bank, :], scalar=gcol[:, 24 + gi:25 + gi], in1=cur["rbc"], op0=ALU.mult, op1=ALU.mult),
              reads=[r_ps[bank], r_gcol, r_rbc], writes=[r_zg])
        tk.op("act", lambda e: e.activation(out=sq[:], in_=zg[:], func=AF.Square, scale=ginv[:, gi:gi + 1]), reads=[r_zg, r_ginv], writes=[r_sq])

    def qk_part_b(zg, r_zg, sq, r_sq, ba, bb):
        mm(ps[:, ba, :], bones[:], sq[:], True, True, [r_bones, r_sq], r_ps[ba])
        mm(ps[:, bb, :], rotT[:], zg[:], True, True, [r_rotT, r_zg], r_ps[bb])

    def qk_part_c(zg, r_zg, ba, bb, outs, r_out):
        t1, r_t1 = nexttmp()
        tk.op("act", lambda e: e.activation(out=t1[:], in_=ps[:, ba, :], func=AF.Ln, bias=epsc[:], scale=1.0 / 64), reads=[r_ps[ba], r_epsc], writes=[r_t1])
        tk.op("act", lambda e: e.activation(out=t1[:], in_=t1[:], func=AF.Exp, scale=-0.5), reads=[r_t1], writes=[r_t1])
        t2, r_t2 = nexttmp()
        tk.op("dve", lambda e: e.tensor_tensor(out=t2[:], in0=ps[:, bb, :], in1=sinb[:], op=ALU.mult), reads=[r_ps[bb], r_sin], writes=[r_t2])
        tk.op("dve", lambda e: e.tensor_tensor(out=zg[:], in0=zg[:], in1=cosb[:], op=ALU.mult), reads=[r_zg, r_cos], writes=[r_zg])
        tk.op("dve", lambda e: e.tensor_tensor(out=t2[:], in0=t2[:], in1=zg[:], op=ALU.add), reads=[r_t2, r_zg], writes=[r_t2])
        for o_ap, rs_ in outs:
            tk.op("dve", lambda e, o_ap=o_ap, rs_=rs_: e.tensor_tensor(out=o_ap, in0=t2[rs_, :], in1=t1[rs_, :], op=ALU.mult), reads=[r_t2, r_t1], writes=[r_out])

    def gate_from(bank, g_ap, r_g):
        tk.op("dve", lambda e: e.tensor_tensor(out=g_ap, in0=ps[:, bank, :], in1=cur["rbc"], op=ALU.mult), reads=[r_ps[bank], r_rbc], writes=[r_g])
        tk.op("act", lambda e: e.activation(out=g_ap, in_=g_ap, func=AF.Silu), reads=[r_g], writes=[r_g])

    def mem_prep(ch, l):
        stage()
        mt = [ovl("memtok%d" % i, [128, D]) for i in range(2)]
        memT, r_memT = ovl("memT", [128, DC, NMEM], BF16, multi=True)
        junk, r_junk = ovl("junk", [128, D])
        ssm, r_ssm = ovl("ssm", [128, 2])
        wk = [ovl("wmk%d" % i, [128, DC, 128], BF16) for i in range(2)]
        wv, r_wv = ovl("wmv", [128, DC, 512], BF16)
        for mtile in range(2):
            m_t, r_m = mt[mtile]
            tk.dma("pool", m_t[:], mem_d[ch, mtile * 128:(mtile + 1) * 128, :], writes=[r_m])
            tk.op("act", lambda e, m_t=m_t, mtile=mtile: e.activation(out=junk[:], in_=m_t[:], func=AF.Square, accum_out=ssm[:, mtile:mtile + 1]),
                  reads=[r_m], writes=[r_junk, r_ssm])
            for c in range(DC):
                bank = 6 + (c % 2)
                tr(ps[:, bank, 0:128], m_t[:, c * 128:(c + 1) * 128], [r_m], r_ps[bank])
                tk.op("dve", lambda e, c=c, bank=bank, mtile=mtile: e.tensor_copy(out=memT[:, c, mtile * 128:(mtile + 1) * 128], in_=ps[:, bank, 0:128]),
                      reads=[r_ps[bank]], writes=[r_memT])
        r_ssm.multi = True
        t1, r_t1 = nexttmp()
        tk.op("act", lambda e: e.activation(out=t1[:, 0:2], in_=ssm[:], func=AF.Ln, bias=epsc[:], scale=1.0 / D), reads=[r_ssm, r_epsc], writes=[r_t1])
        tk.op("act", lambda e: e.activation(out=rsm[:], in_=t1[:, 0:2], func=AF.Exp, scale=-0.5), reads=[r_t1], writes=[r_rsm])
        tk.dma("sp", wv[:], wMV_d[l], reads=[r_wscr], writes=[r_wv])
        for mtile in range(2):
            bank = nextpj()
            for c in range(DC):
                mm(ps[:, bank, :], memT[:, c, mtile * 128:(mtile + 1) * 128], wv[:, c, :], c == 0, c == DC - 1, [r_memT, r_wv], r_ps[bank])
            tk.op("dve", lambda e, bank=bank, mtile=mtile: e.tensor_scalar(out=Vm[:, mtile, :], in0=ps[:, bank, :], scalar1=rsm[:, mtile:mtile + 1], scalar2=None, op0=ALU.mult),
                  reads=[r_ps[bank], r_rsm], writes=[r_Vm])
        for h in range(4):
            wkt, r_wk = wk[h % 2]
            tk.dma("sp", wkt[:], wMK_d[l, h], reads=[r_wscr], writes=[r_wk])
            bank = nextpj()
            for c in range(DC):
                mm(ps[:, bank, 0:NMEM], wkt[:, c, :], memT[:, c, :], c == 0, c == DC - 1, [r_wk, r_memT], r_ps[bank])
            tk.op("act", lambda e, bank=bank, h=h: e.copy(out=KmT[:, h, :], in_=ps[:, bank, 0:NMEM]), reads=[r_ps[bank]], writes=[r_KmT])
        tk.op("dve", lambda e: e.tensor_scalar(out=rsm[:], in0=rsm[:], scalar1=float(128 ** -0.5), scalar2=None, op0=ALU.mult), reads=[r_rsm, r_Vm], writes=[r_rsm])

    def phase1(l, pt):
        stage()
        faT, r_faT = ovl("faT", [128, 4, NB], BF16, multi=True)
        uvo = [ovl("uvo%d" % i, [128, 1024], BF16, multi=True) for i in range(2)]
        vao = [ovl("vao%d" % i, [128, 4, 320], BF16, multi=True) for i in range(2)]
        kto = [ovl("kto%d" % i, [128, NB], BF16) for i in range(2)]
        for i in range(2):
            tk.op("pool", lambda e, i=i: e.memset(vao[i][0][:], 1.0), writes=[vao[i][1]])
        for b in range(NBLK):
            make_xb(b, stats=True)
            load_rope(pt, b)
            ws = wcol_stream(l, [TK, TV] + [TFA + j for j in range(4)])
            wt, r_wt = ws.get(0)
            bank = nextpj()
            proj(wt, r_wt, bank)
            kt, r_kt = kto[b % 2]
            zgk, r_zgk = nexttmp()
            sqk, r_sqk = nextsq()
            qk_part_a(bank, 1, zgk, r_zgk, sqk, r_sqk)
            wt, r_wt = ws.get(1)
            va, r_va = vao[b % 2]
            for tt in range(4):
                bank = nextpj()
                for c in range(DC):
                    mm(ps[:, bank, 0:128], xb[:, c, tt * 128:(tt + 1) * 128], wt[:, c, :], c == 0, c == DC - 1, [r_xb, r_wt], r_ps[bank])
                for kv in range(2):
                    tk.op("dve", lambda e, bank=bank, tt=tt, kv=kv: e.tensor_scalar(out=va[:, tt, 64 + 128 * kv:128 + 128 * kv], in0=ps[:, bank, 64 * kv:64 * kv + 64],
                                                                                 scalar1=cur["rcol"][:, tt:tt + 1], scalar2=None, op0=ALU.mult),
                          reads=[r_ps[bank], r_rcol], writes=[r_va])
            tk.dma("pool", vbuf_d[b].rearrange("(t p) n -> p t n", p=128), va[:], reads=[r_va], writes=[r_vbuf])
            for j in range(4):
                wt, r_wt = ws.get(2 + j)
                bank = nextpj()
                proj(wt, r_wt, bank)
                tk.op("dve", lambda e, bank=bank, j=j: e.tensor_tensor(out=faT[:, j, :], in0=ps[:, bank, :], in1=cur["rbc"], op=ALU.mult),
                      reads=[r_ps[bank], r_rbc], writes=[r_faT])
            qk_part_b(zgk, r_zgk, sqk, r_sqk, 6, 7)
            qk_part_c(zgk, r_zgk, 6, 7, [(kt[:], slice(0, 128))], r_kt)
            tk.dma("pool", kbuf_d[:, b * NB:(b + 1) * NB], kt[:], reads=[r_kt], writes=[r_kbuf])
            for tt in range(4):
                uo, r_uo = uvo[tt % 2]
                for half in range(2):
                    bank = 2 + half
                    for g2 in range(2):
                        g = half * 2 + g2
                        mm(ps[:, bank, g2 * 256:(g2 + 1) * 256], faT[:, g, tt * 128:(tt + 1) * 128], wcws[:, g, :], True, True, [r_faT, r_wcws], r_ps[bank], sig=(g2 == 1))
                    if half == 0:
                        tk.op("act", lambda e, bank=bank, uo=uo: e.copy(out=uo[:, 0:512], in_=ps[:, bank, :]), reads=[r_ps[bank]], writes=[r_uo])
                    else:
                        tk.op("dve", lambda e, bank=bank, uo=uo: e.tensor_copy(out=uo[:, 512:1024], in_=ps[:, bank, :]), reads=[r_ps[bank]], writes=[r_uo])
                tk.dma("pool", uvbuf_d[b][tt * 128:(tt + 1) * 128, :], uo[:], reads=[r_uo], writes=[r_uvbuf])

    cc_sems = [[nc.alloc_semaphore("cc%d" % i), 0] for i in range(1 + 2 * NBLK)] if has_p else []

    def gather_prompt():
        def cc(k, i, o, r_i, r_o):
            toks = tk._deps([r_i], [r_o])
            tk._wait("pool", toks)
            inst = nc.gpsimd.collective_compute("AllGather", ALU.bypass, replica_groups=[[0, 1, 2, 3], [4, 5, 6, 7]], ins=[i.opt()], outs=[o.opt()])
            cc_sems[k][1] += 1
            inst.then_inc(cc_sems[k][0], 1)
            tk._commit([r_i], [r_o], {cc_sems[k][0]: cc_sems[k][1]})
        cc(0, kbuf_d, kg_d, r_kbuf, r_kg)
        for j in range(NBLK):
            cc(1 + j, vbuf_d[j], vg_d[j], r_vbuf, r_vg)
        for j in range(NBLK):
            cc(1 + NBLK + j, uvbuf_d[j], uvg_d[j], r_uvbuf, r_uvg)

    def phase2(l, is_p):
        stage()
        NT = (SP if is_p else T) // 128
        uvi = [ovl("uvi%d" % i, [128, 1024], BF16) for i in range(4)]
        csi = [ovl("csi%d" % i, [128, 2, 2 * NB], BF16) for i in range(4)]
        gf, r_gf = ovl("gf", [128, 2, 4, NB], F32, multi=True)
        cfos = [ovl("cfo%d" % i, [128, 4, NB], BF16, multi=True) for i in range(2)]
        r_uvsrc = r_uvg if is_p else r_uvbuf

        def uv_tile(tt):
            if is_p:
                r, lt = tt // NTT, tt % NTT
                j, i0 = lt // 4, (lt % 4) * 128
                return uvg_d[j][r * NB + i0:r * NB + i0 + 128, :]
            return uvbuf_d[tt // 4][(tt % 4) * 128:(tt % 4 + 1) * 128, :]
        dft_src = dftp_d if is_p else dfts_d
        for kbp in range(NBLK // 2):
            for kb2 in range(2):
                kb = 2 * kbp + kb2
                make_xb(kb)
                ws = wcol_stream(l, [TFG + j for j in range(4)])
                for j in range(4):
                    wt, r_wt = ws.get(j)
                    bank = nextpj()
                    proj(wt, r_wt, bank)
                    gate_from(bank, gf[:, kb2, j, :], r_gf)

            def mk_uv(tt):
                def ld(t, r):
                    tk.dma("sp", t[:], uv_tile(tt), reads=[r_uvsrc], writes=[r])
                return ld

            def mk_cs(tt, kbp=kbp):
                def ld(t, r):
                    tk.dma("sp", t[:], dft_src[:, tt * 128:(tt + 1) * 128, kbp * 2 * NB:(kbp + 1) * 2 * NB].rearrange("k p n -> p k n"), writes=[r])
                return ld
            s_uv = Stream(uvi, [mk_uv(tt) for tt in range(NT)])
            s_cs = Stream(csi, [mk_cs(tt) for tt in range(NT)])
            for tt in range(NT):
                u, r_u = s_uv.get(tt)
                cs, r_c = s_cs.get(tt)
                for kb2 in range(2):
                    for g in range(4):
                        bk = kb2 * 4 + g
                        mm(ps[:, bk, :], u[:, g * 256:g * 256 + 128], cs[:, 0, kb2 * NB:(kb2 + 1) * NB], tt == 0, False, [r_u, r_c], r_ps[bk], sig=False)
                        mm(ps[:, bk, :], u[:, g * 256 + 128:g * 256 + 256], cs[:, 1, kb2 * NB:(kb2 + 1) * NB], False, tt == NT - 1, [r_u, r_c], r_ps[bk],
                           sig=(tt == NT - 1 or (kb2 == 1 and g == 3)))
            for kb2 in range(2):
                kb = 2 * kbp + kb2
                cfo, r_cfo = cfos[kb2]
                for g in range(4):
                    bk = kb2 * 4 + g
                    tk.op("dve", lambda e, g=g, bk=bk, cfo=cfo, kb2=kb2: e.tensor_tensor(out=cfo[:, g, :], in0=ps[:, bk, :], in1=gf[:, kb2, g, :], op=ALU.mult),
                          reads=[r_ps[bk], r_gf], writes=[r_cfo])
                tk.dma("pool", cf_d[:, :, kb * NB:(kb + 1) * NB], cfo[:], reads=[r_cfo], writes=[r_cf])

    def phase3_block(l, b, pt, is_p):
        NKC = (SP if is_p else T) // 128
        if b == 0:
            make_xb(b)
        else:
            cur["rbc"] = rbc_all[:, b, :]
            cur["rcol"] = rcol_all[:, 4 * b:4 * b + 4]
        load_rope(pt, b)
        tk.dma("sp", concat[:, 4:8, :], cf_d[:, :, b * NB:(b + 1) * NB], reads=[r_cf], writes=[r_concat])
        stage()
        qrot, r_qrot0 = ovl("qrot", [128, 8, NB], BF16, multi=True)
        r_qt = [r_qrot0] + [Res("qrot_t%d" % i, multi=True) for i in range(1, 4)]
        for i in range(1, 4):
            r_qt[i].prev = dict(r_qrot0.prev)
            st["live"].append(r_qt[i])
        tk.op("pool", lambda e: e.memset(qrot[:], 0.0), writes=r_qt)
        Ga, r_Ga = ovl("Ga", [128, 4, NB], F32, multi=True)
        NKG_ = ((SP if is_p else T) // 128) // 4
        if is_p:
            kst = [ovl("kst%d" % i, [128, NB], BF16) for i in range(3)]
            vst = [ovl("vst%d" % i, [128, 4, 320], BF16) for i in range(3)]
        else:
            kst = [ovl("kres%d" % i, [128, NB], BF16) for i in range(NKG_)]
            vst = [ovl("vres%d" % i, [128, 4, 320], BF16) for i in range(NKG_)]
        ptl = [ovl("pt%d" % i, [128, 2, NB], BF16) for i in range(4)]
        zgs = [ovl("zgs%d" % i, [128, NB]) for i in range(4)]
        sqs = [ovl("sqs%d" % i, [128, NB], BF16) for i in range(4)]
        rcs = [ovl("rc%d" % i, [128, NB], F32, multi=True) for i in range(2)]
        for i in range(2):
            tk.op("pool", lambda e, i=i: e.memset(rcs[i][0][:], 0.0), writes=[rcs[i][1]])
        ws = wcol_stream(l, [TQ + j for j in range(4)] + [TAG + j for j in range(4)])
        banks = []
        qbanks = [(6, 7), (2, 3), (4, 5), (6, 7)]

        def q_bc(j):
            qk_part_b(zgs[j][0], zgs[j][1], sqs[j][0], sqs[j][1], qbanks[j][0], qbanks[j][1])
            qk_part_c(zgs[j][0], zgs[j][1], qbanks[j][0], qbanks[j][1],
                      [(qrot[0:64, j, :], slice(0, 64)), (qrot[64:128, 4 + j, :], slice(64, 128))], r_qt[j])
        for j in range(4):
            wt, r_wt = ws.get(j)
            bank = nextpj()
            proj(wt, r_wt, bank)
            qk_part_a(bank, 0, zgs[j][0], zgs[j][1], sqs[j][0], sqs[j][1])
            if j >= 1:
                q_bc(j - 1)
        for j in range(4):
            wt, r_wt = ws.get(4 + j)
            bank = nextpj()
            proj(wt, r_wt, bank)
            gate_from(bank, Ga[:, j, :], r_Ga)
            if j == 0:
                q_bc(3)
        k_src, r_ksrc = (kg_d, r_kg) if is_p else (kbuf_d, r_kbuf)
        r_vsrc = r_vg if is_p else r_vbuf

        def v_grp(kgp):
            if is_p:
                return vg_d[kgp % NBLK][(kgp // NBLK) * NB:(kgp // NBLK + 1) * NB, :]
            return vbuf_d[kgp]
        NKG = NKC // 4
        KPQ = T // NB

        def mk_k(kgp):
            def ld(t, r):
                rk, cb = kgp // KPQ, kgp % KPQ
                tk.dma("sp", t[:], k_src[rk * 128:(rk + 1) * 128, cb * NB:(cb + 1) * NB], reads=[r_ksrc], writes=[r])
            return ld

        def mk_v(kgp):
            def ld(t, r):
                tk.dma("sp", t[:], v_grp(kgp).rearrange("(t p) n -> p t n", p=128), reads=[r_vsrc], writes=[r])
            return ld
        sc = float(64 ** -0.5)
        NKP = NKC // 2
        pt_i = [0]
        deferred = []

        def epilogue(h, obank):
            par = h % 2
            hp = h // 2
            orow = slice(0, 64) if par == 0 else slice(64, 128)
            d0 = 64 if par == 0 else 0
            rct, r_rct = rcs[par]
            bb = 6 + par
            tk.op("dve", lambda e: e.reciprocal(out=rct[d0:d0 + 1, :], in_=ps[d0:d0 + 1, obank, :]), reads=[r_ps[obank]], writes=[r_rct])
            selt, r_selt = (sel64, r_sel64) if d0 == 64 else (sel0, r_sel0)
            mm(ps[:, bb, :], selt[:], rct[:], True, True, [r_selt, r_rct], r_ps[bb])
            t1, r_t1 = nexttmp()
            tk.op("dve", lambda e: e.tensor_tensor(out=t1[orow, :], in0=ps[orow, obank, :], in1=Ga[orow, hp, :], op=ALU.mult),
                  reads=[r_ps[obank], r_Ga], writes=[r_t1])
            tk.op("dve", lambda e: e.tensor_tensor(out=concat[orow, hp, :], in0=t1[orow, :], in1=ps[orow, bb, :], op=ALU.mult),
                  reads=[r_t1, r_ps[bb]], writes=[r_concat])

        slots = [(0, 1), (2, 3), (6, 7)]
        steps = [(h, kp) for h in range(8) for kp in range(NKP)]
        NST = len(steps)
        LOOK = 2
        if is_p:
            s_k = Stream(kst, [mk_k(i % NKG) for i in range(8 * NKG)])
            s_v = Stream(vst, [mk_v(i % NKG) for i in range(8 * NKG)])
        else:
            s_k = Stream(kst, [mk_k(i) for i in range(NKG)])
            s_v = Stream(vst, [mk_v(i) for i in range(NKG)])
        pbuf = {}
        for g in range(NST + LOOK):
            if g < NST:
                h, kp = steps[g]
                s0, s1 = slots[g % 3]
                ktile, r_k = s_k.get((h * NKG if is_p else 0) + kp // 2)
                for i in range(2):
                    kc = 2 * kp + i
                    sb = s0 + i
                    mm(ps[:, sb, :], ktile[:, (kc % 4) * 128:(kc % 4 + 1) * 128], qrot[:, h, :], True, True, [r_k, r_qt[h % 4]], r_ps[sb])
                p_t, r_p = ptl[g % 4]
                tk.op("act", lambda e, p_t=p_t, s0=s0: e.activation(out=p_t[:], in_=ps[:, s0:s0 + 2, :], func=AF.Exp, scale=sc),
                      reads=[r_ps[s0], r_ps[s1]], writes=[r_p])
                pbuf[g] = (p_t, r_p)
            g0 = g - LOOK
            if g0 >= 0:
                h0, kp0 = steps[g0]
                kv0 = h0 // 4
                par0 = h0 % 2
                vcol = (64 if par0 == 0 else 0) + 128 * kv0
                obank = 4 + par0
                p0, r_p0 = pbuf.pop(g0)
                vtile, r_v = s_v.get((h0 * NKG if is_p else 0) + kp0 // 2)
                for i in range(2):
                    kc0 = 2 * kp0 + i
                    mm(ps[:, obank, :], vtile[:, kc0 % 4, vcol:vcol + 128], p0[:, i, :], kc0 == 0, kc0 == NKC - 1, [r_v, r_p0], r_ps[obank],
                       sig=(i == 1))
                if kp0 == NKP - 1:
                    deferred.append((h0, obank))
                if kp0 == 0 and deferred and deferred[0][0] != h0:
                    epilogue(*deferred.pop(0))
        while deferred:
            epilogue(*deferred.pop(0))
        stage()
        vh, r_vh = ovl("vh", [128, 4, 512], BF16, multi=True)
        SG, r_SG = ovl("SG", [128, 4, NB], F32, multi=True)
        svn, r_svn = ovl("svn", [128, 512])
        junk, r_junk = ovl("junk", [128, 128])
        ss4, r_ss4 = ovl("ss4", [128, 4], F32, multi=True)
        rs4, r_rs4 = ovl("rs4", [128, 4])
        wsv, r_wsv = wbig[0]
        tk.dma("sp", wsv[:], wSV_d[l], reads=[r_wscr], writes=[r_wsv])
        for tt in range(4):
            bank = nextpj()
            for c in range(DC):
                mm(ps[:, bank, :], xb[:, c, tt * 128:(tt + 1) * 128], wsv[:, c, :], c == 0, c == DC - 1, [r_xb, r_wsv], r_ps[bank])
            tk.op("dve", lambda e, bank=bank, tt=tt: e.tensor_scalar(out=svn[:], in0=ps[:, bank, :], scalar1=cur["rcol"][:, tt:tt + 1], scalar2=None, op0=ALU.mult),
                  reads=[r_ps[bank], r_rcol], writes=[r_svn])
            for h in range(4):
                tk.op("act", lambda e, h=h: e.activation(out=junk[:], in_=svn[:, h * 128:(h + 1) * 128], func=AF.Square, accum_out=ss4[:, h:h + 1]),
                      reads=[r_svn], writes=[r_junk, r_ss4])
            tk.op("act", lambda e: e.activation(out=rs4[:], in_=ss4[:], func=AF.Ln, bias=epsc[:], scale=1.0 / 128), reads=[r_ss4, r_epsc], writes=[r_rs4])
            tk.op("act", lambda e: e.activation(out=rs4[:], in_=rs4[:], func=AF.Exp, scale=-0.5), reads=[r_rs4], writes=[r_rs4])
            for h in range(4):
                tk.op("dve", lambda e, h=h, tt=tt: e.scalar_tensor_tensor(out=vh[:, tt, h * 128:(h + 1) * 128], in0=svn[:, h * 128:(h + 1) * 128], scalar=rs4[:, h:h + 1],
                                                                       in1=vgb[:, h * 128:(h + 1) * 128], op0=ALU.mult, op1=ALU.mult),
                      reads=[r_svn, r_rs4, r_vgb], writes=[r_vh])
        mqT, r_mqT = ovl("mqT", [128, 4, NB], BF16, multi=True)
        Gm, r_Gm = ovl("Gm", [128, 4, NB], F32, multi=True)
        pm = [ovl("pm%d" % i, [128, NB], BF16) for i in range(4)]
        ws = wcol_stream(l, [TSU + j for j in range(4)] + [TSG + j for j in range(4)])
        for j in range(4):
            wt, r_wt = ws.get(j)
            bank = nextpj()
            proj(wt, r_wt, bank)
            tk.op("dve", lambda e, bank=bank, j=j: e.tensor_tensor(out=SG[:, j, :], in0=ps[:, bank, :], in1=cur["rbc"], op=ALU.mult), reads=[r_ps[bank], r_rbc], writes=[r_SG])
        for j in range(4):
            wt, r_wt = ws.get(4 + j)
            bank = nextpj()
            proj(wt, r_wt, bank)
            t1, r_t1 = nexttmp()
            gate_from(bank, t1[:], r_t1)
            tk.op("pool", lambda e, j=j, t1=t1: e.tensor_tensor(out=SG[:, j, :], in0=SG[:, j, :], in1=t1[:], op=ALU.mult), reads=[r_t1, r_SG], writes=[r_SG])
        ws = wcol_stream(l, [TMQ + j for j in range(4)] + [TMG + j for j in range(4)])
        for j in range(4):
            wt, r_wt = ws.get(j)
            bank = nextpj()
            proj(wt, r_wt, bank)
            tk.op("dve", lambda e, bank=bank, j=j: e.tensor_tensor(out=mqT[:, j, :], in0=ps[:, bank, :], in1=cur["rbc"], op=ALU.mult), reads=[r_ps[bank], r_rbc], writes=[r_mqT])
        for j in range(4):
            wt, r_wt = ws.get(4 + j)
            bank = nextpj()
            proj(wt, r_wt, bank)
            gate_from(bank, Gm[:, j, :], r_Gm)

        def m_qk(h):
            pts = []
            for mc in range(2):
                sb = 2 + mc
                mm(ps[:, sb, :], KmT[:, h, mc * 128:(mc + 1) * 128], mqT[:, h, :], True, True, [r_KmT, r_mqT], r_ps[sb])
                p_t, r_p = pm[(2 * h + mc) % 4]
                tk.op("act", lambda e, p_t=p_t, sb=sb, mc=mc: e.activation(out=p_t[:], in_=ps[:, sb, :], func=AF.Exp, scale=rsm[:, mc:mc + 1]), reads=[r_ps[sb], r_rsm], writes=[r_p])
                pts.append((p_t, r_p))
            return pts

        def sgu_head(h):
            bank = nextpj()
            for tt in range(4):
                mm(ps[:, bank, tt * 128:(tt + 1) * 128], vh[:, tt, h * 128:(h + 1) * 128], wsT[:, h, :], True, True, [r_vh, r_wsT], r_ps[bank], sig=(tt == 3))
            t1, r_t1 = nexttmp()
            tk.op("dve", lambda e, t1=t1: e.tensor_tensor(out=t1[:], in0=ps[:, bank, :], in1=bsb[:, h, :], op=ALU.add), reads=[r_ps[bank], r_bsb], writes=[r_t1])
            tk.op("dve", lambda e, t1=t1: e.tensor_tensor(out=concat[:, 8 + h, :], in0=t1[:], in1=SG[:, h, :], op=ALU.mult), reads=[r_t1, r_SG], writes=[r_concat])

        def m_pv(h, pts):
            ob, db = (4, 5) if h % 2 == 0 else (6, 7)
            for mc in range(2):
                p_t, r_p = pts[mc]
                mm(ps[:, ob, :], Vm[:, mc, h * 128:(h + 1) * 128], p_t[:], mc == 0, mc == 1, [r_Vm, r_p], r_ps[ob])
                mm(ps[:, db, :], onesb[:], p_t[:], mc == 0, mc == 1, [r_onesb, r_p], r_ps[db])
            t1, r_t1 = nexttmp()
            tk.op("act", lambda e, t1=t1: e.activation(out=t1[:], in_=ps[:, db, :], func=AF.Ln), reads=[r_ps[db]], writes=[r_t1])
            tk.op("act", lambda e, t1=t1: e.activation(out=t1[:], in_=t1[:], func=AF.Exp, scale=-1.0), reads=[r_t1], writes=[r_t1])
            tk.op("dve", lambda e, t1=t1: e.tensor_tensor(out=t1[:], in0=t1[:], in1=Gm[:, h, :], op=ALU.mult), reads=[r_t1, r_Gm], writes=[r_t1])
            tk.op("dve", lambda e, t1=t1: e.tensor_tensor(out=concat[:, 12 + h, :], in0=ps[:, ob, :], in1=t1[:], op=ALU.mult), reads=[r_ps[ob], r_t1], writes=[r_concat])

        for h in range(4):
            pts = m_qk(h)
            sgu_head(h)
            m_pv(h, pts)
        stage()
        oT, r_oT = ovl("oT", [128, DC, NB], F32, multi=True)
        wo = [ovl("wo%d" % i, [128, 16, 128], BF16) for i in range(3)]

        def mk_wo(dt):
            def ld(t, r):
                tk.dma("sp", t[:], wO_d[l, dt], reads=[r_wscr], writes=[r])
            return ld
        s_wo = Stream(wo, [mk_wo(dt) for dt in range(8)])
        pend_ss = None
        for dt in range(8):
            wt, r_wt = s_wo.get(dt)
            bank = nextpj()
            for ct in range(16):
                mm(ps[:, bank, :], wt[:, ct, :], concat[:, ct, :], ct == 0, ct == 15, [r_wt, r_concat], r_ps[bank])
            tk.op("act", lambda e, bank=bank, dt=dt: e.copy(out=oT[:, dt, :], in_=ps[:, bank, :]), reads=[r_ps[bank]], writes=[r_oT])
            sq, r_sq = nextsq()
            tk.op("dve", lambda e, sq=sq, dt=dt: e.tensor_tensor(out=sq[:], in0=oT[:, dt, :], in1=oT[:, dt, :], op=ALU.mult), reads=[r_oT], writes=[r_sq])
            if pend_ss is not None:
                mm(ps[:, 6, :], onesb[:], pend_ss[0][:], pend_ss[2] == 0, False, [r_onesb, pend_ss[1]], r_ps[6], sig=True)
            pend_ss = (sq, r_sq, dt)
        mm(ps[:, 6, :], onesb[:], pend_ss[0][:], False, True, [r_onesb, pend_ss[1]], r_ps[6])
        if b + 1 < NBLK:
            make_xb(b + 1)
        t1, r_t1 = nexttmp()
        tk.op("act", lambda e: e.activation(out=t1[:], in_=ps[:, 6, :], func=AF.Ln, bias=epsc[:], scale=1.0 / D), reads=[r_ps[6], r_epsc], writes=[r_t1])
        tk.op("act", lambda e: e.activation(out=t1[:], in_=t1[:], func=AF.Exp, scale=-0.5), reads=[r_t1], writes=[r_t1])
        for dt in range(8):
            eng = "dve" if dt % 2 == 0 else "pool"
            tk.op("dve", lambda e, dt=dt: e.scalar_tensor_tensor(out=oT[:, dt, :], in0=oT[:, dt, :], scalar=gcol[:, 8 + dt:9 + dt], in1=t1[:], op0=ALU.mult, op1=ALU.mult),
                  reads=[r_oT, r_gcol, r_t1], writes=[r_oT])
            tk.op(eng, lambda e, dt=dt: e.tensor_tensor(out=xT[:, dt, b * NB:(b + 1) * NB], in0=xT[:, dt, b * NB:(b + 1) * NB], in1=oT[:, dt, :], op=ALU.add),
                  reads=[r_oT, r_xTb[b]], writes=[r_xTb[b]])

    def load_chunk(ch):
        stage()
        tok = [ovl("tok%d" % i, [128, D]) for i in range(2)]
        for tt in range(NTT):
            t_t, r_t = tok[tt % 2]
            tk.dma("sp", t_t[:], x_d[ch, tt * 128:(tt + 1) * 128, :], writes=[r_t])
            for half in range(2):
                bank = nextpj()
                for c4 in range(4):
                    c = half * 4 + c4
                    tr(ps[:, bank, c4 * 128:(c4 + 1) * 128], t_t[:, c * 128:(c + 1) * 128], [r_t], r_ps[bank], sig=(c4 == 3))
                eng = "dve" if half == 0 else "act"
                if eng == "dve":
                    tk.op("dve", lambda e, bank=bank, half=half, tt=tt: e.tensor_copy(out=xT[:, half * 4:half * 4 + 4, tt * 128:(tt + 1) * 128],
                                                                                     in_=ps[:, bank, :].rearrange("p (c n) -> p c n", c=4)),
                          reads=[r_ps[bank]], writes=[r_xTb[tt // 4]])
                else:
                    tk.op("act", lambda e, bank=bank, half=half, tt=tt: e.copy(out=xT[:, half * 4:half * 4 + 4, tt * 128:(tt + 1) * 128],
                                                                               in_=ps[:, bank, :].rearrange("p (c n) -> p c n", c=4)),
                          reads=[r_ps[bank]], writes=[r_xTb[tt // 4]])

    def store_chunk(ch):
        stage()
        tok = [ovl("otok%d" % i, [128, D], F32, multi=True) for i in range(2)]
        for tt in range(NTT):
            t_t, r_t = tok[tt % 2]
            for half in range(2):
                bank = nextpj()
                for c4 in range(4):
                    c = half * 4 + c4
                    tr(ps[:, bank, c4 * 128:(c4 + 1) * 128], xT[:, c, tt * 128:(tt + 1) * 128], [r_xTb[tt // 4]], r_ps[bank], sig=(c4 == 3))
                if half == 0:
                    tk.op("dve", lambda e, bank=bank, t_t=t_t: e.tensor_copy(out=t_t[:, 0:512], in_=ps[:, bank, :]), reads=[r_ps[bank]], writes=[r_t])
                else:
                    tk.op("act", lambda e, bank=bank, t_t=t_t: e.copy(out=t_t[:, 512:1024], in_=ps[:, bank, :]), reads=[r_ps[bank]], writes=[r_t])
            tk.dma("pool", y_d[ch, tt * 128:(tt + 1) * 128, :], t_t[:], reads=[r_t], writes=[r_y])

    for ch in range(NCH):
        is_p = has_p and ch == 0
        pt = 0 if is_p else 1
        load_chunk(ch)
        for l in range(L):
            load_layer_consts(l)
            phase1(l, pt)
            if is_p:
                gather_prompt()
            mem_prep(ch, l)
            phase2(l, is_p)
            for b in range(NBLK):
                phase3_block(l, b, pt, is_p)
        store_chunk(ch)
    tk.wait_all("sp", [r_y])
    tk.wait_all("pool", [r_y])
    return nc, tk, st


def _rope_tables(pos0, T):
    t = np.arange(pos0, pos0 + T)
    row = (t // 64).astype(np.float32)
    col = (t % 64).astype(np.float32)
    inv = (np.float32(10000.0) ** (-np.arange(16, dtype=np.float32) / np.float32(16))).astype(np.float32)
    ang_r = row[:, None] * inv[None, :]
    ang_c = col[:, None] * inv[None, :]
    out = np.zeros((2, 128, T), np.float32)
    for p in range(128):
        d = p % 64
        half, j = d // 32, d % 32
        i = j % 16
        a = (ang_r if half == 0 else ang_c)[:, i]
        out[0, p] = np.cos(a)
        out[1, p] = np.sin(a)
    return out


def _consts():
    c = np.zeros((5, 128, 128), np.float32)
    c[0] = np.eye(128, dtype=np.float32)
    R = np.zeros((128, 128), np.float32)
    for p in range(128):
        d = p % 64
        j = d % 32
        if j < 16:
            R[p, p + 16] = -1.0
        else:
            R[p, p - 16] = 1.0
    c[1] = R.T
    c[2] = np.kron(np.eye(2, dtype=np.float32), np.ones((64, 64), np.float32))
    k = np.arange(128)
    ang = 2.0 * np.pi * ((k[:, None] * k[None, :]) % 128) / 128.0
    c[3] = (np.cos(ang) / np.sqrt(128.0)).astype(np.float32)
    c[4] = (np.sin(ang) / np.sqrt(128.0)).astype(np.float32)
    return c


def _dft(S, k0, nk):
    t = np.arange(S, dtype=np.int64)[:, None]
    k = np.arange(k0, k0 + nk, dtype=np.int64)[None, :]
    ang = 2.0 * np.pi * ((t * k) % S).astype(np.float64) / S
    out = np.empty((2, S, nk), ml_dtypes.bfloat16)
    out[0] = (np.cos(ang) / np.sqrt(S)).astype(np.float32).astype(ml_dtypes.bfloat16)
    out[1] = (-np.sin(ang) / np.sqrt(S)).astype(np.float32).astype(ml_dtypes.bfloat16)
    return out


def _host_layout(cfg, x_chunks, mem_chunks, pos0s, w, core):
    L = cfg["depth"]
    T = cfg["T"]
    g = np.zeros((L, 128, 32), np.float32)
    for l in range(L):
        g[l, :, 0:8] = w["pre_norm_g"][l].reshape(8, 128).T
        g[l, :, 8:16] = w["post_norm_g"][l].reshape(8, 128).T
        g[l, :, 16:24] = w["mem_norm_g"][l].reshape(8, 128).T
        g[l, :, 24] = np.tile(w["q_norm_g"][l], 2)
        g[l, :, 25] = np.tile(w["k_norm_g"][l], 2)
        g[l, :, 26:] = 1.0
    vgb = np.ascontiguousarray(np.broadcast_to(w["sgu_norm_g"].reshape(L, 1, 512), (L, 128, 512)))
    bsb = np.ascontiguousarray(np.broadcast_to(np.tile(w["b_spatial"].reshape(L, 1, 4, 128), (1, 1, 1, NB // 128)), (L, 128, 4, NB)))
    m = {
        "x": np.ascontiguousarray(x_chunks), "mem": np.ascontiguousarray(mem_chunks),
        "w_in": w["w_in"], "w_out": w["w_out"], "w_mem_kv": w["w_mem_kv"], "w_fourier": w["w_fourier"],
        "w_spatial": w["w_spatial"], "gcols": g, "vgb": vgb, "bsb": bsb, "cst": _consts(),
        "rope": np.stack([_rope_tables(pos0s[0], T), _rope_tables(pos0s[1], T)]),
        "dft_s": _DFT_CACHE.setdefault(("s", T), _dft(T, 0, T)),
    }
    if cfg["prompt"]:
        q = core % 4
        m["dft_p"] = _DFT_CACHE.setdefault(("p", T, q), _dft(4 * T, q * T, T))
    return m


_DFT_CACHE = {}
_NC_CACHE = {}


def kernel(x_prompt, x_sample, mem_prompt, mem_sample, pre_norm_g, w_in, q_norm_g, k_norm_g, w_fourier,
           sgu_norm_g, w_spatial, b_spatial, mem_norm_g, w_mem_kv, w_out, post_norm_g):
    f = lambda a: np.ascontiguousarray(np.asarray(a, dtype=np.float32))
    w = dict(pre_norm_g=f(pre_norm_g), w_in=f(w_in), q_norm_g=f(q_norm_g), k_norm_g=f(k_norm_g), w_fourier=f(w_fourier),
             sgu_norm_g=f(sgu_norm_g), w_spatial=f(w_spatial), b_spatial=f(b_spatial), mem_norm_g=f(mem_norm_g),
             w_mem_kv=f(w_mem_kv), w_out=f(w_out), post_norm_g=f(post_norm_g))
    x_prompt, x_sample, mem_prompt, mem_sample = f(x_prompt), f(x_sample), f(mem_prompt), f(mem_sample)
    cfg = dict(depth=2, T=2048, n_samp=4, prompt=True)
    T = cfg["T"]
    if "nc" not in _NC_CACHE:
        _NC_CACHE["nc"] = build(cfg)[0]
    nc = _NC_CACHE["nc"]
    in_maps = []
    for c in range(8):
        pb, q = c // 4, c % 4
        xs = np.concatenate([x_prompt[pb, q * T:(q + 1) * T][None], x_sample[4 * c:4 * c + 4]], axis=0)
        ms = np.concatenate([mem_prompt[pb][None], mem_sample[4 * c:4 * c + 4]], axis=0)
        in_maps.append(_host_layout(cfg, xs, ms, (q * T, 0), w, c))
    res = run_bass_kernel_spmd(nc, in_maps, core_ids=list(range(8)))
    y_prompt = np.empty_like(x_prompt)
    y_sample = np.empty_like(x_sample)
    for c in range(8):
        y = res.results[c]["y"]
        pb, q = c // 4, c % 4
        y_prompt[pb, q * T:(q + 1) * T] = y[0]
        y_sample[4 * c:4 * c + 4] = y[1:]
    return (y_prompt, y_sample)
```
